# Optimizing a Trainium2 kernel written in Bass

```python
import jax, jax.numpy as jnp
from jax import lax
import numpy as np

D_MODEL = 1024
BATCH = 4
SEQ = 8192
DEPTH = 2

N_MIXERS = 2
EXPAND = 2
D_INNER = EXPAND * D_MODEL
CONV_WIDTH = 3
CHUNK = 128
GMLP_GROUPS = 8
GROUP_WIDTH = D_INNER // GMLP_GROUPS
N_CONV_LAYERS = (DEPTH + 1) // 2
N_GMLP_LAYERS = DEPTH // 2
RMS_EPS = 1e-6
LN_EPS = 1e-5

kernel_name = "hybrid_shortconv_chunked_gmlp_adaln"


def rms_norm(x, g):
    xf = x.astype(jnp.float32)
    y = xf * lax.rsqrt(jnp.mean(xf * xf, axis=-1, keepdims=True) + RMS_EPS)
    return (y * g.astype(jnp.float32)).astype(x.dtype)


def layer_norm(x, g, b):
    xf = x.astype(jnp.float32)
    mu = jnp.mean(xf, axis=-1, keepdims=True)
    var = jnp.mean(jnp.square(xf - mu), axis=-1, keepdims=True)
    y = (xf - mu) * lax.rsqrt(var + LN_EPS)
    return (y * g.astype(jnp.float32) + b.astype(jnp.float32)).astype(x.dtype)


def short_conv_mixer(h, w_in, conv_w, conv_b, w_out):
    seq = h.shape[1]
    proj = h @ w_in
    b_gate, c_gate, xin, z = jnp.split(proj, 4, axis=-1)
    cx = c_gate * xin
    padded = jnp.pad(cx, ((0, 0), (CONV_WIDTH - 1, 0), (0, 0)))
    conv = conv_b + conv_w[CONV_WIDTH - 1] * cx
    for k in range(CONV_WIDTH - 1):
        conv = conv + conv_w[k] * padded[:, k:k + seq]
    y = jax.nn.silu(z) * b_gate * conv
    return y @ w_out


def chunked_gmlp_mixer(h, w_in, ln_g, ln_b, w_s, b_s, w_out):
    bsz, seq, _ = h.shape
    proj = h @ w_in
    uv, z = proj[..., :2 * D_INNER], proj[..., 2 * D_INNER:]
    u, v = jnp.split(jax.nn.gelu(uv, approximate=False), 2, axis=-1)
    v = layer_norm(v, ln_g, ln_b)
    n_chunks = seq // CHUNK
    v = v.reshape(bsz, n_chunks, CHUNK, GMLP_GROUPS, GROUP_WIDTH)
    causal = jnp.tril(jnp.ones((CHUNK, CHUNK), dtype=bool))
    w = jnp.where(causal[None], w_s, jnp.zeros_like(w_s)).astype(v.dtype)
    mixed = jnp.einsum('gts,bnsgc->bntgc', w, v)
    mixed = mixed + jnp.transpose(b_s)[None, None, :, :, None].astype(v.dtype)
    s = u * mixed.reshape(bsz, seq, D_INNER)
    y = jax.nn.silu(z) * s
    return y @ w_out


def setup_inputs(seed: int = 0) -> dict:
    key = jax.random.key(seed)
    ks = jax.random.split(key, 20)
    nrm = jax.random.normal
    d, e = D_MODEL, D_INNER
    return {
        "x": nrm(ks[0], (BATCH, SEQ, d), jnp.float32),
        "c": nrm(ks[1], (BATCH, d), jnp.float32),
        "mod_w": nrm(ks[2], (DEPTH, d, 3 * d), jnp.float32) * (0.5 * d ** -0.5),
        "mod_b": nrm(ks[3], (DEPTH, 3 * d), jnp.float32) * 0.02,
        "norm_g": 1.0 + 0.05 * nrm(ks[4], (DEPTH, d), jnp.float32),
        "a_w_in": nrm(ks[5], (N_CONV_LAYERS, d, 4 * e), jnp.float32) * d ** -0.5,
        "a_conv_w": nrm(ks[6], (N_CONV_LAYERS, CONV_WIDTH, e), jnp.float32) * CONV_WIDTH ** -0.5,
        "a_conv_b": nrm(ks[7], (N_CONV_LAYERS, e), jnp.float32) * 0.02,
        "a_w_out": nrm(ks[8], (N_CONV_LAYERS, e, d), jnp.float32) * e ** -0.5,
        "b_w_in": nrm(ks[9], (N_GMLP_LAYERS, d, 3 * e), jnp.float32) * d ** -0.5,
        "b_ln_g": 1.0 + 0.05 * nrm(ks[10], (N_GMLP_LAYERS, e), jnp.float32),
        "b_ln_b": 0.02 * nrm(ks[11], (N_GMLP_LAYERS, e), jnp.float32),
        "b_w_s": nrm(ks[12], (N_GMLP_LAYERS, GMLP_GROUPS, CHUNK, CHUNK), jnp.float32) * CHUNK ** -0.5,
        "b_b_s": 1.0 + 0.1 * nrm(ks[13], (N_GMLP_LAYERS, GMLP_GROUPS, CHUNK), jnp.float32),
        "b_w_out": nrm(ks[14], (N_GMLP_LAYERS, e, d), jnp.float32) * e ** -0.5,
        "final_g": 1.0 + 0.05 * nrm(ks[15], (d,), jnp.float32),
    }


def reference(x, c, mod_w, mod_b, norm_g, a_w_in, a_conv_w, a_conv_b, a_w_out,
              b_w_in, b_ln_g, b_ln_b, b_w_s, b_b_s, b_w_out, final_g):
    c_act = jax.nn.silu(c)
    for i in range(DEPTH):
        mod = c_act @ mod_w[i] + mod_b[i]
        shift, scale, gate = jnp.split(mod[:, None, :], 3, axis=-1)
        h = rms_norm(x, norm_g[i]) * (1.0 + scale) + shift
        j = i // N_MIXERS
        if i % N_MIXERS == 0:
            branch = short_conv_mixer(h, a_w_in[j], a_conv_w[j], a_conv_b[j], a_w_out[j])
        else:
            branch = chunked_gmlp_mixer(h, b_w_in[j], b_ln_g[j], b_ln_b[j],
                                        b_w_s[j], b_b_s[j], b_w_out[j])
        x = x + gate * branch
    return rms_norm(x, final_g)
```

```python
import numpy as np
from contextlib import ExitStack
import concourse.bass as bass
import concourse.mybir as mybir
from concourse.bass_utils import run_bass_kernel_spmd

F32 = mybir.dt.float32
BF16 = mybir.dt.bfloat16
AF = mybir.ActivationFunctionType
ALU = mybir.AluOpType
AX = mybir.AxisListType

ENGS = ("tensor", "vector", "scalar", "gpsimd", "sync")
NST = 4
DBG = None
T = 1024
RMS_EPS = 1e-6
LN_EPS = 1e-5


class Ev:
    __slots__ = ("kind", "eng", "seq", "sem", "val", "clk", "gi")

    def __init__(self, kind, eng=None, seq=None, sem=None, val=None, clk=None, gi=0):
        self.kind, self.eng, self.seq, self.sem, self.val, self.clk, self.gi = kind, eng, seq, sem, val, clk, gi


class Res:
    __slots__ = ("name", "w", "r")

    def __init__(self, name):
        self.name, self.w, self.r = name, None, []


class DmaSem:
    __slots__ = ("idx", "count")

    def __init__(self, idx):
        self.idx, self.count = idx, 0


class Op:
    __slots__ = ("fn", "deps", "ev", "dma")

    def __init__(self, fn, deps, ev, dma):
        self.fn, self.deps, self.ev, self.dma = fn, deps, ev, dma


class Plan:
    def __init__(self):
        self.ops = {e: [] for e in ENGS}
        self.dmasems = []
        self.known = {e: {f: -1 for f in ENGS} for e in ENGS}
        self.knownd = {e: {} for e in ENGS}
        self.gi = 0

    def dma_sem(self):
        s = DmaSem(len(self.dmasems))
        self.dmasems.append(s)
        return s

    def op(self, eng, fn, reads=(), writes=(), dma=None):
        deps = []
        for r in reads:
            if r.w is not None:
                deps.append(r.w)
        for w in writes:
            if w.w is not None:
                deps.append(w.w)
            deps.extend(w.r)
        kn, knd = self.known[eng], self.knownd[eng]
        fdeps = []
        need_drain = False
        for d in sorted(set(deps), key=lambda d: -d.gi):
            if d.kind == "eng":
                if d.eng == eng:
                    if eng in ("vector", "scalar") and d.seq > kn[eng]:
                        need_drain = True
                    continue
                if d.seq <= kn[d.eng]:
                    continue
                fdeps.append(Ev("eng", eng=d.eng, seq=d.seq))
                kn[d.eng] = d.seq
            else:
                if d.val <= knd.get(d.sem.idx, 0):
                    continue
                fdeps.append(Ev("dma", sem=d.sem, val=d.val))
                knd[d.sem.idx] = d.val
            if d.clk is not None:
                for f, v in d.clk.items():
                    if f != eng and v > kn[f]:
                        kn[f] = v
        seq = len(self.ops[eng])
        if need_drain:
            fdeps.append(Ev("eng", eng=eng, seq=seq - 1))
            kn[eng] = seq - 1
        self.gi += 1
        clk = {f: v for f, v in kn.items() if f != eng and v >= 0}
        if dma is None:
            clk[eng] = seq
            ev = Ev("eng", eng=eng, seq=seq, clk=clk, gi=self.gi)
        else:
            dma.count += 16
            if kn[eng] >= 0:
                clk[eng] = kn[eng]
            ev = Ev("dma", sem=dma, val=dma.count, clk=clk, gi=self.gi)
        self.ops[eng].append(Op(fn, fdeps, ev, dma))
        for r in reads:
            r.r.append(ev)
        for w in writes:
            w.w = ev
            w.r = []
        return ev

    def emit(self, nc, final_waits=()):
        sig = {e: set() for e in ENGS}
        for e in ENGS:
            for o in self.ops[e]:
                for d in o.deps:
                    if d.kind == "eng" and d.eng != e:
                        sig[d.eng].add(d.seq)
        rank = {e: {s: i + 1 for i, s in enumerate(sorted(sig[e]))} for e in ENGS}
        with ExitStack() as st:
            esem = {e: st.enter_context(nc.semaphore("s_" + e)) for e in ENGS}
            dsem = [st.enter_context(nc.semaphore("d%d" % i)) for i in range(len(self.dmasems))]
            block = st.enter_context(nc.Block())

            def run(e, engobj):
                for seq, o in enumerate(self.ops[e]):
                    for d in o.deps:
                        if d.kind == "eng" and d.eng == e:
                            engobj.drain()
                        elif d.kind == "eng":
                            engobj.wait_ge(esem[d.eng], rank[d.eng][d.seq])
                        else:
                            engobj.wait_ge(dsem[d.sem.idx], d.val)
                    inst = o.fn(engobj)
                    if o.dma is not None:
                        inst.then_inc(dsem[o.dma.idx], 16)
                    elif seq in rank[e]:
                        inst.then_inc(esem[e], 1)
                if e == "sync":
                    for d in final_waits:
                        engobj.wait_ge(dsem[d.sem.idx], d.val)

            @block.tensor
            def _(eng):
                run("tensor", eng)

            @block.vector
            def _(eng):
                run("vector", eng)

            @block.scalar
            def _(eng):
                run("scalar", eng)

            @block.gpsimd
            def _(eng):
                run("gpsimd", eng)

            @block.sync
            def _(eng):
                run("sync", eng)


def build_nc():
    nc = bass.Bass("TRN2", target_bir_lowering=False)

    def din(name, shape):
        return nc.dram_tensor(name, list(shape), F32, kind="ExternalInput").ap()

    x_d = din("x", [NST * T, 1024])
    xh_d = din("xh", [128, 1024])
    hmask_d = din("hmask", [128, 1])
    ct_d = din("c_t", [128, 8])
    modw_d = din("modw", [2, 6, 128, 4096])
    modbf_d = din("modb_f", [128, 48])
    modbg_d = din("modb_g", [1, 2048])
    normg_d = din("normg_f", [128, 16])
    w0in_d = din("w0in", [16, 128, 4096])
    w0out_d = din("w0out", [4, 128, 4096])
    convw_d = din("convw", [128, 48])
    convb_d = din("convb", [128, 16])
    w1uz_d = din("w1uz", [16, 128, 2048])
    w1v_d = din("w1v", [4, 128, 4096])
    w1out_d = din("w1out", [4, 128, 4096])
    lng_d = din("lng", [128, 16])
    lnb_d = din("lnb", [128, 16])
    wsT_d = din("wsT", [128, 1024])
    trilm_d = din("trilm", [128, 128])
    bsbc_d = din("bs_bc", [128, 1024])
    fgbc_d = din("fg_bc", [128, 1024])
    ident_d = din("ident", [128, 128])
    out_d = nc.dram_tensor("out", [NST * T, 1024], F32, kind="ExternalOutput").ap()

    with ExitStack() as st:
        st.enter_context(nc.allow_low_precision("bf16 matmul operands, fp32 accumulation"))

        def sb(name, shape, dt):
            return st.enter_context(nc.sbuf_tensor(name, list(shape), dt))

        X = [sb("X%d" % j, [128, 1024], F32) for j in range(8)]
        hT = sb("hT", [128, 8, T], BF16)
        YT = [sb("YT%d" % i, [128, 4, T], BF16) for i in range(2)]
        RING = [sb("RING%d" % i, [128, 8, 512], BF16) for i in range(4)]
        WO = [sb("WO%d" % i, [128, 4, 1024], BF16) for i in range(2)]
        VH = sb("VH", [128, 8, 2048], BF16)
        XN = [sb("XN%d" % i, [128, 1024], BF16) for i in range(4)]
        PST = [sb("PST%d" % i, [128, 1024], F32) for i in range(2)]
        TA = [sb("TA%d" % i, [128, 512], F32) for i in range(2)]
        TB = [sb("TB%d" % i, [128, 516], F32) for i in range(2)]
        TC = [sb("TC%d" % i, [128, 512], F32) for i in range(2)]
        TD = [sb("TD%d" % i, [128, 512], F32) for i in range(2)]
        GATE = sb("GATE", [128, 2, 1024], F32)
        FG = sb("FG", [128, 1024], F32)
        T2 = sb("T2", [128, 16, 128], F32)
        WT = sb("WT", [128, 1024], BF16)
        IDB = sb("IDB", [128, 128], BF16)
        IDF = sb("IDF", [128, 128], F32)
        TRIL = sb("TRIL", [128, 128], F32)
        ONESF = sb("ONESF", [128, 128], F32)
        CT = sb("CT", [128, 8], F32)
        CACT = sb("CACT", [128, 8], BF16)
        MODBF = sb("MODBF", [128, 48], F32)
        NORMG = sb("NORMG", [128, 16], F32)
        CONVW = sb("CONVW", [128, 48], F32)
        CONVB = sb("CONVB", [128, 16], F32)
        LNG = sb("LNG", [128, 16], F32)
        LNB = sb("LNB", [128, 16], F32)
        GP = sb("GP", [128, 16], F32)
        HMASK = sb("HMASK", [128, 1], F32)
        AM = sb("AM", [128, 2, 8], F32)
        SH = sb("SH", [128, 2, 8], F32)
        TM16 = sb("TM16", [128, 16], F32)
        HTH = sb("HTH", [128, 8, 2], BF16)
        TAIL = sb("TAIL", [128, 16, 2], F32)
        CH = sb("CH", [128, 2], F32)
        SS = sb("SS", [128, 4], F32)
        SQ = sb("SQ", [128, 4], F32)
        RS = sb("RS", [128, 4], F32)
        S1 = sb("S1", [128, 32], F32)
        S2 = sb("S2", [128, 32], F32)
        LNS = sb("LNS", [128, 6, 8], F32)
        EPS = sb("EPS", [128, 2], F32)
        SSN = sb("SSN", [128, 8], F32)
        SQN = sb("SQN", [128, 8], F32)
        RSN = sb("RSN", [128, 8], F32)
        SS1 = sb("SS1", [128, 8], F32)
        SQ1 = sb("SQ1", [128, 8], F32)
        RS1 = sb("RS1", [128, 8], F32)
        PS = [st.enter_context(nc.psum_tensor("PS%d" % i, [128, 512], F32)) for i in range(8)]

        P = Plan()
        rX = [[Res("X%d_%d" % (j, d)) for d in range(2)] for j in range(8)]
        rhT = [Res("hT0"), Res("hT1")]
        rhTx = [Res("hT0x"), Res("hT1x")]
        rYT = [Res("YT0"), Res("YT1")]
        rRING = [Res("R%d" % i) for i in range(4)]
        rWO = [Res("WO0"), Res("WO1")]
        rVH = [[Res("VH%d_%d" % (j, c)) for c in range(2)] for j in range(8)]
        rPST = [Res("PST0"), Res("PST1")]
        rSS1 = [Res("SS1a"), Res("SS1b")]
        rSQ1 = [Res("SQ1a"), Res("SQ1b")]
        rRS1 = [Res("RS1a"), Res("RS1b")]
        rSSN, rSQN, rRSN, rJ = Res("SSN"), Res("SQN"), Res("RSN"), Res("JUNK")
        rS2 = Res("S2")
        rXN = [Res("XN%d" % i) for i in range(4)]
        rTA = [Res("TA0"), Res("TA1")]
        rTB = [Res("TB0"), Res("TB1")]
        rTC = [Res("TC0"), Res("TC1")]
        rTD = [Res("TD0"), Res("TD1")]
        rPS = [Res("PS%d" % i) for i in range(8)]
        rC = Res("consts")
        rTAIL, rCH = Res("TAIL"), Res("CH")
        rSS, rSQ, rRS, rS1, rLNS = Res("SS"), Res("SQ"), Res("RS"), Res("S1"), Res("LNS")
        rHTH = Res("HTH")
        rGATE = Res("GATE")
        rMOD = Res("mod")
        rTM16 = Res("TM16")

        dX = [P.dma_sem() for _ in range(8)]
        dRING = [P.dma_sem() for _ in range(4)]
        dWO = [P.dma_sem() for _ in range(2)]
        dPST = [P.dma_sem() for _ in range(2)]

        def cload(dst, src, res_list):
            s = P.dma_sem()
            P.op("sync", lambda e: e.dma_start(out=dst, in_=src), writes=res_list, dma=s)

        SCR = YT[1][:].rearrange("p a b -> p (a b)").bitcast(F32)
        WS_F, BS_BC = SCR[:, 0:1024], SCR[:, 1024:2048]
        GROW = VH[0:1, 0, :].bitcast(F32)
        MODBG = [VH[0:1, 1 + l, :].bitcast(F32) for l in range(2)]

        rc = {k: Res(k) for k in ["CT", "MODBF", "MODBG", "NORMG", "CONVW", "CONVB", "LNG", "LNB", "HMASK", "FG", "TRIL", "IDF"]}
        cload(CT[:], ct_d, [rc["CT"]])
        cload(MODBF[:], modbf_d, [rc["MODBF"]])
        cload(NORMG[:], normg_d, [rc["NORMG"]])
        cload(IDF[:], ident_d, [rc["IDF"]])
        for j in range(8):
            P.op("sync", lambda e, j=j: e.dma_start(out=X[j][:], in_=x_d[j * 128:(j + 1) * 128, :]), writes=rX[j], dma=dX[j])
        cload(PST[0][:], xh_d, [rPST[0]])
        cload(HMASK[:], hmask_d, [rc["HMASK"]])
        for l in range(2):
            cload(MODBG[l], modbg_d[:, l * 1024:(l + 1) * 1024], [rc["MODBG"]] + rVH[1 + l])
        cload(TRIL[:], trilm_d, [rc["TRIL"]])
        cload(WS_F, wsT_d, [rYT[1]])
        cload(BS_BC, bsbc_d, [rYT[1]])
        cload(CONVW[:], convw_d, [rc["CONVW"]])
        cload(CONVB[:], convb_d, [rc["CONVB"]])
        cload(LNG[:], lng_d, [rc["LNG"]])
        cload(LNB[:], lnb_d, [rc["LNB"]])
        cload(FG[:], fgbc_d, [rc["FG"]])

        rONES, rIDB, rEPS, rCACT = Res("ONES"), Res("IDB"), Res("EPS"), Res("CACT")
        P.op("gpsimd", lambda e: e.memset(ONESF[:], 1.0), writes=[rONES])

        def eps_set(e):
            e.memset(EPS[:, 0:1], RMS_EPS)
            return e.memset(EPS[:, 1:2], LN_EPS)
        P.op("gpsimd", eps_set, writes=[rEPS])
        P.op("vector", lambda e: e.tensor_copy(out=IDB[:], in_=IDF[:]), reads=[rc["IDF"]], writes=[rIDB])
        P.op("scalar", lambda e: e.activation(out=CACT[:], in_=CT[:], func=AF.Silu), reads=[rc["CT"]], writes=[rCACT])

        MOD_INS = {1: (0, 4), 2: (0, 5), 4: (1, 0), 6: (1, 1), 8: (1, 2), 10: (1, 3), 12: (1, 4), 14: (1, 5)}
        ring_state = {"issued": 0}
        ring_loads = []

        def ring_full(tag, src):
            return (tag, lambda s: RING[s][:], src.rearrange("p (k c) -> p k c", k=8))

        def ring_half(tag, src):
            return (tag, lambda s: RING[s][:, :, 0:256], src.rearrange("p (k c) -> p k c", k=8))

        for n in range(4):
            ring_loads.append(ring_full(("mod", 0, n), modw_d[0, n]))
        for s_ in range(NST):
            for f in range(16):
                if s_ == 0 and f in MOD_INS:
                    l_, n_ = MOD_INS[f]
                    ring_loads.append(ring_full(("mod", l_, n_), modw_d[l_, n_]))
                ring_loads.append(ring_full(("w0", s_, f), w0in_d[f]))
            for n in range(4):
                ring_loads.append(ring_full(("wv", s_, n), w1v_d[n]))
            for f in range(16):
                ring_loads.append(ring_half(("wuz", s_, f), w1uz_d[f]))

        def ring_ensure(upto):
            upto = min(upto, len(ring_loads) - 1)
            while ring_state["issued"] <= upto:
                i = ring_state["issued"]
                s = i % 4
                _, dstf, src = ring_loads[i]
                P.op("gpsimd", lambda e, dstf=dstf, src=src, s=s: e.dma_start(out=dstf(s), in_=src),
                     writes=[rRING[s]], dma=dRING[s])
                ring_state["issued"] += 1

        ring_pos = {"i": 0}

        def ring_take(tag, ahead=3):
            i = ring_pos["i"]
            assert ring_loads[i][0] == tag, (ring_loads[i][0], tag)
            ring_ensure(i + ahead)
            ring_pos["i"] += 1
            return i % 4

        wo_state = {"n": 0}

        def wo_load(src):
            b = wo_state["n"] % 2
            wo_state["n"] += 1
            P.op("gpsimd", lambda e: e.dma_start(out=WO[b][:], in_=src.rearrange("p (f d) -> p f d", f=4)),
                 writes=[rWO[b]], dma=dWO[b])
            return b

        def wo_scale_slice(b, l, fi):
            P.op("gpsimd", lambda e: e.tensor_tensor(out=WO[b][:, fi, :], in0=WO[b][:, fi, :], in1=GATE[:, l, :], op=ALU.mult),
                 reads=[rGATE], writes=[rWO[b]])

        WO_SLICE_AT = {3: 0, 4: 1, 5: 2, 6: 3}

        def mod_step(l, n, fb, gb):
            s = ring_take(("mod", l, n))
            if n < 4:
                def mm_fm(e, s=s, n=n):
                    last = None
                    for jc in range(4):
                        col = n * 4 + jc
                        for k in range(8):
                            last = e.matmul(PS[fb][:, col:col + 1], lhsT=RING[s][:, k, jc * 128:(jc + 1) * 128],
                                            rhs=CACT[:, k:k + 1], start=(k == 0), stop=(k == 7))
                    return last
                P.op("tensor", mm_fm, reads=[rRING[s], rCACT], writes=[rPS[fb]])
                if n == 3:
                    P.op("vector", lambda e: e.tensor_tensor(out=TM16[:], in0=PS[fb][:, 0:16], in1=MODBF[:, l * 24:l * 24 + 16], op=ALU.add),
                         reads=[rPS[fb], rc["MODBF"]], writes=[rTM16])
                    P.op("vector", lambda e: e.tensor_copy(out=SH[:, l, :], in_=TM16[:, 0:8]), reads=[rTM16], writes=[rMOD])
                    P.op("vector", lambda e: e.scalar_tensor_tensor(out=AM[:, l, :], in0=TM16[:, 8:16], scalar=1.0,
                                                                    in1=NORMG[:, l * 8:(l + 1) * 8], op0=ALU.add, op1=ALU.mult),
                         reads=[rTM16, rc["NORMG"]], writes=[rMOD])
            else:
                dh = n - 4

                def mm_g(e, s=s):
                    last = None
                    for k in range(8):
                        last = e.matmul(PS[gb][0:1, :], lhsT=CACT[:, k:k + 1], rhs=RING[s][:, k, :], start=(k == 0), stop=(k == 7))
                    return last
                P.op("tensor", mm_g, reads=[rRING[s], rCACT], writes=[rPS[gb]])
                P.op("vector", lambda e: e.tensor_tensor(out=GROW[:, dh * 512:(dh + 1) * 512], in0=PS[gb][0:1, :],
                                                         in1=MODBG[l][:, dh * 512:(dh + 1) * 512], op=ALU.add),
                     reads=[rPS[gb], rc["MODBG"]] + rVH[1 + l], writes=rVH[0])
                P.op("tensor", lambda e: e.matmul(PS[gb][:], lhsT=ONESF[0:1, :], rhs=GROW[:, dh * 512:(dh + 1) * 512], start=True, stop=True),
                     reads=[rONES] + rVH[0], writes=[rPS[gb]])
                P.op("scalar", lambda e: e.activation(out=GATE[:, l, dh * 512:(dh + 1) * 512], in_=PS[gb][:], func=AF.Copy),
                     reads=[rPS[gb]], writes=[rGATE])

        rWT = Res("WT")
        rT2 = Res("T2")

        def l1_consts():
            P.op("vector", lambda e: e.tensor_tensor(out=WS_F.rearrange("p (g t) -> p g t", g=8), in0=WS_F.rearrange("p (g t) -> p g t", g=8),
                                                     in1=TRIL[:].unsqueeze(1).to_broadcast([128, 8, 128]), op=ALU.mult),
                 reads=[rc["TRIL"]], writes=[rYT[1]])
            P.op("vector", lambda e: e.tensor_copy(out=WT[:], in_=WS_F), reads=[rYT[1]], writes=[rWT])
            for hh in range(2):
                P.op("tensor", lambda e, hh=hh: e.matmul(PS[7][:], lhsT=ONESF[:], rhs=WS_F[:, hh * 512:(hh + 1) * 512], start=True, stop=True),
                     reads=[rONES, rYT[1]], writes=[rPS[7]])

                def t2_build(e, hh=hh):
                    last = None
                    for f in range(hh * 8, hh * 8 + 8):
                        g = f // 2
                        off = (g % 4) * 128
                        last = e.scalar_tensor_tensor(out=T2[:, f, :], in0=PS[7][:, off:off + 128], scalar=LNB[:, f:f + 1],
                                                      in1=BS_BC[:, g * 128:(g + 1) * 128], op0=ALU.mult, op1=ALU.add)
                    return last
                P.op("vector", t2_build, reads=[rPS[7], rc["LNB"], rYT[1]], writes=[rT2])
            P.op("vector", lambda e: e.tensor_scalar(out=T2[:].rearrange("p f t -> p (f t)"), in0=T2[:].rearrange("p f t -> p (f t)"),
                                                     scalar1=0.5, scalar2=None, op0=ALU.mult), reads=[rT2], writes=[rT2])
            P.op("vector", lambda e: e.tensor_scalar(out=GP[:], in0=LNG[:], scalar1=0.5, scalar2=None, op0=ALU.mult), reads=[rc["LNG"]], writes=[rT2])

        def tpv(b):
            return PS[b][:].bitcast(BF16).rearrange("p (k t) -> p k t", k=2)

        HXN, rHXN = VH[:, 7, 0:1024], rVH[7][0]

        ST0_BANKS = [0, 1, 2, 3, 4, 5, 0, 1]

        def st0_slot(j):
            return (XN[j][:], rXN[j]) if j < 4 else (WO[1][:, j - 4, :], rWO[1])

        def st0_tr(g):
            hf, kp, b = g // 4, g % 4, ST0_BANKS[g]

            def tr(e):
                last = None
                for k2 in range(2):
                    k = kp * 2 + k2
                    for jj in range(4):
                        src, _ = st0_slot(hf * 4 + jj)
                        last = e.transpose(out=tpv(b)[:, k2, jj * 128:(jj + 1) * 128], in_=src[:, k * 128:(k + 1) * 128], identity=IDB[:])
                return last
            P.op("tensor", tr, reads=[st0_slot(hf * 4 + jj)[1] for jj in range(4)] + [rIDB], writes=[rPS[b]])

        def st0_ev(g):
            hf, kp, b = g // 4, g % 4, ST0_BANKS[g]
            k = kp * 2
            P.op("scalar", lambda e: e.activation(out=hT[:, k, hf * 512:(hf + 1) * 512], in_=tpv(b)[:, 0, :], func=AF.Identity,
                                                  scale=AM[:, 0, k:k + 1], bias=SH[:, 0, k:k + 1]),
                 reads=[rPS[b], rMOD], writes=[rhT[hf]])
            P.op("vector", lambda e: e.tensor_scalar(out=hT[:, k + 1, hf * 512:(hf + 1) * 512], in0=tpv(b)[:, 1, :], scalar1=AM[:, 0, k + 1:k + 2],
                                                     scalar2=SH[:, 0, k + 1:k + 2], op0=ALU.mult, op1=ALU.add),
                 reads=[rPS[b], rMOD], writes=[rhTx[hf]])

        def st0_h_phase():
            for hf in range(2):
                for jj in range(4):
                    j = hf * 4 + jj
                    dst, rdst = st0_slot(j)
                    P.op("scalar", lambda e, j=j, dst=dst: e.activation(out=dst, in_=X[j][:], func=AF.Square, accum_out=SS1[:, j:j + 1]),
                         reads=rX[j], writes=[rdst, rSS1[hf]])
                P.op("scalar", lambda e, hf=hf: e.activation(out=SQ1[:, hf * 4:hf * 4 + 4], in_=SS1[:, hf * 4:hf * 4 + 4], func=AF.Sqrt,
                                                             bias=EPS[:, 0:1], scale=1.0 / 1024),
                     reads=[rSS1[hf], rEPS], writes=[rSQ1[hf]])
            for hf in range(2):
                P.op("vector", lambda e, hf=hf: e.reciprocal(out=RS1[:, hf * 4:hf * 4 + 4], in_=SQ1[:, hf * 4:hf * 4 + 4]),
                     reads=[rSQ1[hf]], writes=[rRS1[hf]])
                for jj in range(4):
                    j = hf * 4 + jj
                    dst, rdst = st0_slot(j)
                    P.op("vector", lambda e, j=j, dst=dst: e.tensor_scalar(out=dst, in0=X[j][:], scalar1=RS1[:, j:j + 1], scalar2=None, op0=ALU.mult),
                         reads=rX[j] + [rRS1[hf]], writes=[rdst])
            for g in range(6):
                st0_tr(g)
            halo_pre()
            for n in range(4):
                mod_step(0, n, 6, None)
            halo_ev_op()
            st0_ev(0)
            st0_ev(1)
            st0_tr(6)
            st0_tr(7)
            for g in range(2, 6):
                st0_ev(g)
            st0_ev(6)
            st0_ev(7)

        def halo_pre():
            P.op("scalar", lambda e: e.activation(out=HXN[:], in_=PST[0][:], func=AF.Square, accum_out=SS[:, 0:1]),
                 reads=[rPST[0]], writes=[rHXN, rSS])
            P.op("scalar", lambda e: e.activation(out=SQ[:, 0:1], in_=SS[:, 0:1], func=AF.Sqrt, bias=EPS[:, 0:1], scale=1.0 / 1024),
                 reads=[rSS, rEPS], writes=[rSQ])
            P.op("vector", lambda e: e.reciprocal(out=RS[:, 0:1], in_=SQ[:, 0:1]), reads=[rSQ], writes=[rRS])
            P.op("vector", lambda e: e.tensor_scalar(out=HXN[:], in0=PST[0][:], scalar1=RS[:, 0:1], scalar2=None, op0=ALU.mult),
                 reads=[rPST[0], rRS], writes=[rHXN])

            def halo_tr(e):
                last = None
                for k in range(8):
                    last = e.transpose(out=tpv(7)[:, k // 4, (k % 4) * 128:(k % 4 + 1) * 128], in_=HXN[:, k * 128:(k + 1) * 128], identity=IDB[:])
                return last
            P.op("tensor", halo_tr, reads=[rHXN, rIDB], writes=[rPS[7]])

        def halo_ev_op():
            def halo_ev(e):
                last = None
                for k in range(8):
                    last = e.activation(out=HTH[:, k, :], in_=tpv(7)[:, k // 4, (k % 4) * 128 + 126:(k % 4) * 128 + 128],
                                        func=AF.Identity, scale=AM[:, 0, k:k + 1], bias=SH[:, 0, k:k + 1])
                return last
            P.op("scalar", halo_ev, reads=[rPS[7], rMOD], writes=[rHTH])

        cnt = {"pair": 0, "ou": 0, "q": 0, "mix": 0, "tp": 0, "a": 0, "stg": 0, "pst": 0}
        STG = [(TA[0], rTA[0]), (TC[0], rTC[0]), (TD[0], rTD[0]), (TA[1], rTA[1]), (TC[1], rTC[1]), (TD[1], rTD[1])]
        dOUT = [P.dma_sem() for _ in STG]

        def h_phase(stx, l, final=False):
            for hf in range(2):
                for jj in range(4):
                    j = hf * 4 + jj
                    if l == 0 and not final:
                        r0 = stx * T + j * 128
                        P.op("sync", lambda e, j=j, r0=r0: e.dma_start(out=X[j][:], in_=x_d[r0:r0 + 128, :]),
                             writes=rX[j], dma=dX[j])
                    P.op("scalar", lambda e, j=j, jj=jj: e.activation(out=XN[jj][:], in_=X[j][:], func=AF.Square, accum_out=SS[:, jj:jj + 1]),
                         reads=rX[j], writes=[rXN[jj], rSS])
                P.op("scalar", lambda e: e.activation(out=SQ[:], in_=SS[:], func=AF.Sqrt, bias=EPS[:, 0:1], scale=1.0 / 1024),
                     reads=[rSS, rEPS], writes=[rSQ])
                P.op("vector", lambda e: e.reciprocal(out=RS[:], in_=SQ[:]), reads=[rSQ], writes=[rRS])
                if final:
                    for jj in range(4):
                        j = hf * 4 + jj
                        r0 = stx * T + j * 128
                        for dh in range(2):
                            stg, rstg = STG[cnt["stg"] % len(STG)]
                            cnt["stg"] += 1
                            P.op("vector", lambda e, j=j, jj=jj, dh=dh, stg=stg: e.scalar_tensor_tensor(
                                out=stg[:, 0:512], in0=X[j][:, dh * 512:(dh + 1) * 512], scalar=RS[:, jj:jj + 1], in1=FG[:, dh * 512:(dh + 1) * 512],
                                op0=ALU.mult, op1=ALU.mult), reads=rX[j] + [rRS, rc["FG"]], writes=[rstg])
                            P.op("sync", lambda e, r0=r0, dh=dh, stg=stg: e.dma_start(out=out_d[r0:r0 + 128, dh * 512:(dh + 1) * 512], in_=stg[:, 0:512]),
                                 reads=[rstg], dma=dOUT[(cnt["stg"] - 1) % len(STG)])
                    continue
                for jj in range(4):
                    j = hf * 4 + jj
                    P.op("vector", lambda e, j=j, jj=jj: e.tensor_scalar(out=XN[jj][:], in0=X[j][:], scalar1=RS[:, jj:jj + 1], scalar2=None, op0=ALU.mult),
                         reads=rX[j] + [rRS], writes=[rXN[jj]])
                for kp in range(4):
                    b = 6 + cnt["tp"] % 2
                    cnt["tp"] += 1

                    def tr(e, kp=kp, b=b):
                        last = None
                        for k2 in range(2):
                            k = kp * 2 + k2
                            for jj in range(4):
                                last = e.transpose(out=tpv(b)[:, k2, jj * 128:(jj + 1) * 128], in_=XN[jj][:, k * 128:(k + 1) * 128], identity=IDB[:])
                        return last
                    P.op("tensor", tr, reads=rXN + [rIDB], writes=[rPS[b]])

                    def ev(e, kp=kp, b=b, hf=hf, l=l):
                        last = None
                        for k2 in range(2):
                            k = kp * 2 + k2
                            last = e.activation(out=hT[:, k, hf * 512:(hf + 1) * 512], in_=tpv(b)[:, k2, :], func=AF.Identity,
                                                scale=AM[:, l, k:k + 1], bias=SH[:, l, k:k + 1])
                        return last
                    P.op("scalar", ev, reads=[rPS[b], rMOD], writes=[rhT[hf]])

        def next_tile_prep(nstx, f):
            if f < 7:
                return
            if f == 12:
                P.op("scalar", lambda e: e.activation(out=SQN[:], in_=SSN[:], func=AF.Sqrt, bias=EPS[:, 0:1], scale=1.0 / 1024),
                     reads=[rSSN, rEPS], writes=[rSQN])
                P.op("vector", lambda e: e.reciprocal(out=RSN[:], in_=SQN[:]), reads=[rSQN], writes=[rRSN])
            if f >= 8:
                for t in range(2):
                    j = ((f - 8) % 4) * 2 + t
                    if f < 12:
                        P.op("scalar", lambda e, t=t, j=j: e.activation(out=VH[:, j, 0:1024], in_=PST[t][:], func=AF.Square, accum_out=SSN[:, j:j + 1]),
                             reads=[rPST[t]], writes=[rVH[j][0], rSSN])
                    else:
                        P.op("vector", lambda e, t=t, j=j: e.tensor_scalar(out=VH[:, j, 0:1024], in0=PST[t][:], scalar1=RSN[:, j:j + 1], scalar2=None,
                                                                        op0=ALU.mult),
                             reads=[rPST[t], rRSN], writes=[rVH[j][0]])
            if f <= 14:
                for t in range(2):
                    j = ((f + 1 - 8) % 4) * 2 + t
                    r0 = nstx * T + j * 128
                    P.op("sync", lambda e, t=t, r0=r0: e.dma_start(out=PST[t][:], in_=x_d[r0:r0 + 128, :]), writes=[rPST[t]], dma=dPST[t])

        def h_tr_next(hf, l=0):
            for kp in range(4):
                b = 6 + cnt["tp"] % 2
                cnt["tp"] += 1

                def tr(e, kp=kp, b=b):
                    last = None
                    for k2 in range(2):
                        k = kp * 2 + k2
                        for jj in range(4):
                            last = e.transpose(out=tpv(b)[:, k2, jj * 128:(jj + 1) * 128], in_=VH[:, hf * 4 + jj, k * 128:(k + 1) * 128], identity=IDB[:])
                    return last
                P.op("tensor", tr, reads=[rVH[hf * 4 + jj][0] for jj in range(4)] + [rIDB], writes=[rPS[b]])

                def ev(e, kp=kp, b=b):
                    last = None
                    for k2 in range(2):
                        k = kp * 2 + k2
                        last = e.activation(out=hT[:, k, hf * 512:(hf + 1) * 512], in_=tpv(b)[:, k2, :], func=AF.Identity,
                                            scale=AM[:, l, k:k + 1], bias=SH[:, l, k:k + 1])
                    return last
                P.op("scalar", ev, reads=[rPS[b], rMOD], writes=[rhT[hf]])

        def h_phase_l1():
            for hf in range(2):
                for jj in range(4):
                    j = hf * 4 + jj
                    P.op("scalar", lambda e, j=j: e.activation(out=VH[:, j, 1024:2048], in_=X[j][:], func=AF.Square, accum_out=SS1[:, j:j + 1]),
                         reads=rX[j], writes=[rVH[j][1], rSS1[hf]])
                P.op("scalar", lambda e, hf=hf: e.activation(out=SQ1[:, hf * 4:hf * 4 + 4], in_=SS1[:, hf * 4:hf * 4 + 4], func=AF.Sqrt,
                                                             bias=EPS[:, 0:1], scale=1.0 / 1024),
                     reads=[rSS1[hf], rEPS], writes=[rSQ1[hf]])
            for hf in range(2):
                P.op("vector", lambda e, hf=hf: e.reciprocal(out=RS1[:, hf * 4:hf * 4 + 4], in_=SQ1[:, hf * 4:hf * 4 + 4]),
                     reads=[rSQ1[hf]], writes=[rRS1[hf]])
                for jj in range(4):
                    j = hf * 4 + jj
                    P.op("vector", lambda e, j=j: e.tensor_scalar(out=VH[:, j, 0:1024], in0=X[j][:], scalar1=RS1[:, j:j + 1], scalar2=None, op0=ALU.mult),
                         reads=rX[j] + [rRS1[hf]], writes=[rVH[j][0]])
                h_tr_next(hf, 1)

        def final_norm(stx_done, reload_stx, halves=(0, 1)):
            for hf in halves:
                for jj in range(4):
                    j = hf * 4 + jj
                    P.op("scalar", lambda e, j=j, jj=jj: e.activation(out=XN[jj][:], in_=X[j][:], func=AF.Square, accum_out=SS[:, jj:jj + 1]),
                         reads=rX[j], writes=[rXN[jj], rSS])
                P.op("scalar", lambda e: e.activation(out=SQ[:], in_=SS[:], func=AF.Sqrt, bias=EPS[:, 0:1], scale=1.0 / 1024),
                     reads=[rSS, rEPS], writes=[rSQ])
                P.op("vector", lambda e: e.reciprocal(out=RS[:], in_=SQ[:]), reads=[rSQ], writes=[rRS])
                for jj in range(4):
                    j = hf * 4 + jj
                    P.op("vector", lambda e, j=j, jj=jj: e.scalar_tensor_tensor(out=X[j][:], in0=X[j][:], scalar=RS[:, jj:jj + 1], in1=FG[:],
                                                                               op0=ALU.mult, op1=ALU.mult),
                         reads=[rRS, rc["FG"]], writes=rX[j])
                for jj in range(4):
                    j = hf * 4 + jj
                    r0 = stx_done * T + j * 128
                    P.op("sync", lambda e, j=j, r0=r0: e.dma_start(out=out_d[r0:r0 + 128, :], in_=X[j][:]), reads=rX[j], dma=dX[j])
                if reload_stx is not None:
                    for jj in range(4):
                        j = hf * 4 + jj
                        r1 = reload_stx * T + j * 128
                        P.op("sync", lambda e, j=j, r1=r1: e.dma_start(out=X[j][:], in_=x_d[r1:r1 + 128, :]), writes=rX[j], dma=dX[j])

        def out_unit(l, yb, wb, j, dh):
            ob = 4 + cnt["ou"] % 2
            cnt["ou"] += 1

            def mm(e):
                last = None
                for fi in range(4):
                    last = e.matmul(PS[ob][:], lhsT=YT[yb][:, fi, j * 128:(j + 1) * 128], rhs=WO[wb][:, fi, dh * 512:(dh + 1) * 512],
                                    start=(fi == 0), stop=(fi == 3))
                return last
            P.op("tensor", mm, reads=[rYT[yb], rWO[wb]], writes=[rPS[ob]])
            P.op("vector", lambda e: e.tensor_tensor(out=X[j][:, dh * 512:(dh + 1) * 512], in0=X[j][:, dh * 512:(dh + 1) * 512], in1=PS[ob][:], op=ALU.add),
                 reads=[rPS[ob]], writes=[rX[j][dh]])

        def layer0(stx, carry):
            pend = []
            wb_cur = None
            for f in range(16):
                grp, fi = f // 4, f % 4
                if fi == 0:
                    wb_cur = wo_load(w0out_d[grp])
                if stx == 0 and f == 3:
                    l1_consts()
                if stx == 0 and f in MOD_INS:
                    l_, n_ = MOD_INS[f]
                    mod_step(l_, n_, 7, 4 + (n_ - 4) if l_ == 0 else 7)
                s = ring_take(("w0", stx, f))
                yb = grp % 2
                if stx == 0:
                    def hm(e, s=s):
                        last = None
                        for bi, blk in ((0, 1), (1, 2)):
                            for k in range(8):
                                last = e.matmul(PS[6][:, bi * 2:bi * 2 + 2], lhsT=RING[s][:, k, blk * 128:(blk + 1) * 128], rhs=HTH[:, k, :],
                                                start=(k == 0), stop=(k == 7))
                        return last
                    P.op("tensor", hm, reads=[rRING[s], rHTH], writes=[rPS[6]])
                    P.op("scalar", lambda e: e.activation(out=CH[:], in_=PS[6][:, 0:2], func=AF.Copy), reads=[rPS[6]], writes=[rCH])
                    P.op("vector", lambda e, f=f: e.scalar_tensor_tensor(out=TAIL[:, f, :], in0=PS[6][:, 2:4], scalar=HMASK[:, 0:1], in1=CH[:],
                                                                          op0=ALU.mult, op1=ALU.mult),
                         reads=[rPS[6], rCH, rc["HMASK"]], writes=[rTAIL])
                for hf in range(2):
                    q = cnt["q"] % 2
                    cnt["q"] += 1
                    for pi, blks in enumerate((((0, 1), (1, 2)), ((0, 0), (1, 3)))):
                        sl = cnt["pair"] % 2
                        cnt["pair"] += 1

                        def mm(e, s=s, sl=sl, blks=blks, hf=hf):
                            last = None
                            for bi, blk in blks:
                                for k in range(8):
                                    last = e.matmul(PS[sl * 2 + bi][:], lhsT=RING[s][:, k, blk * 128:(blk + 1) * 128],
                                                    rhs=hT[:, k, hf * 512:(hf + 1) * 512], start=(k == 0), stop=(k == 7))
                            return last
                        P.op("tensor", mm, reads=[rRING[s], rhT[hf], rhTx[hf]], writes=[rPS[sl * 2], rPS[sl * 2 + 1]])
                        p0, p1 = sl * 2, sl * 2 + 1
                        if pi == 0:
                            P.op("scalar", lambda e, q=q, p0=p0: e.activation(out=TA[q][:], in_=PS[p0][:], func=AF.Copy),
                                 reads=[rPS[p0]], writes=[rTA[q]])
                            P.op("scalar", lambda e, q=q, f=f: e.activation(out=TB[q][:, 0:2], in_=TAIL[:, f, :], func=AF.Copy),
                                 reads=[rTAIL], writes=[rTB[q]])
                            P.op("vector", lambda e, q=q, p1=p1: e.tensor_tensor(out=TB[q][:, 2:514], in0=PS[p1][:], in1=TA[q][:], op=ALU.mult),
                                 reads=[rPS[p1], rTA[q]], writes=[rTB[q]])
                            P.op("scalar", lambda e, q=q, f=f: e.activation(out=TAIL[:, f, :], in_=TB[q][:, 512:514], func=AF.Copy),
                                 reads=[rTB[q]], writes=[rTAIL])
                            P.op("scalar", lambda e, q=q, f=f: e.activation(out=TC[q][:], in_=TB[q][:, 2:514], func=AF.Identity,
                                                                            scale=CONVW[:, f * 3 + 2:f * 3 + 3], bias=CONVB[:, f:f + 1]),
                                 reads=[rTB[q], rc["CONVW"], rc["CONVB"]], writes=[rTC[q]])

                            P.op("vector", lambda e, q=q, f=f: e.scalar_tensor_tensor(out=TC[q][:], in0=TB[q][:, 1:513], scalar=CONVW[:, f * 3 + 1:f * 3 + 2],
                                                                                     in1=TC[q][:], op0=ALU.mult, op1=ALU.add),
                                 reads=[rTB[q], rTC[q], rc["CONVW"]], writes=[rTC[q]])
                            P.op("vector", lambda e, q=q, f=f: e.scalar_tensor_tensor(out=TC[q][:], in0=TB[q][:, 0:512], scalar=CONVW[:, f * 3:f * 3 + 1],
                                                                                     in1=TC[q][:], op0=ALU.mult, op1=ALU.add),
                                 reads=[rTB[q], rTC[q], rc["CONVW"]], writes=[rTC[q]])
                        else:
                            P.op("scalar", lambda e, q=q, p1=p1: e.activation(out=TD[q][:], in_=PS[p1][:], func=AF.Silu),
                                 reads=[rPS[p1]], writes=[rTD[q]])
                            P.op("vector", lambda e, q=q, p0=p0: e.tensor_tensor(out=TD[q][:], in0=PS[p0][:], in1=TD[q][:], op=ALU.mult),
                                 reads=[rPS[p0], rTD[q]], writes=[rTD[q]])
                            P.op("gpsimd", lambda e, q=q, yb=yb, fi=fi, hf=hf: e.tensor_tensor(out=YT[yb][:, fi, hf * 512:(hf + 1) * 512], in0=TD[q][:], in1=TC[q][:],
                                                                                               op=ALU.mult),
                                 reads=[rTD[q], rTC[q]], writes=[rYT[yb]])
                        if pi == 1:
                            u = fi * 2 + hf
                            if stx == 0 and grp == 0:
                                for sl_ in {5: (0, 1), 6: (2, 3)}.get(u, ()):
                                    wo_scale_slice(wb_cur, 0, sl_)
                            elif u in WO_SLICE_AT:
                                wo_scale_slice(wb_cur, 0, WO_SLICE_AT[u])
                        if carry:
                            for _ in range(2):
                                if carry:
                                    out_unit(1, *carry.pop(0))
                            if len(carry) == 8:
                                final_norm(stx - 1, stx, halves=(0,))
                            if not carry:
                                final_norm(stx - 1, stx, halves=(1,))
                        elif pend:
                            out_unit(0, *pend.pop(0))
                if fi == 3:
                    assert not pend
                    pend = [(yb, wb_cur, j, dh) for j in range(8) for dh in range(2)]
            assert not carry
            while pend:
                out_unit(0, *pend.pop(0))

        rLNSh = [Res("LNSa"), Res("LNSb")]

        def ln_half(h):
            c0, c1 = 4 * h, 4 * h + 4
            rl = rLNSh[h]
            P.op("vector", lambda e: e.tensor_reduce(out=LNS[:, 0, c0:c1], in_=S1[:, c0 * 4:c1 * 4].rearrange("p (j n) -> p j n", n=4), axis=AX.X, op=ALU.add),
                 reads=[r for j in range(c0, c1) for r in rVH[j]], writes=[rl, rVH[c1 - 1][1]])
            P.op("vector", lambda e: e.tensor_reduce(out=LNS[:, 1, c0:c1], in_=S2[:, c0 * 4:c1 * 4].rearrange("p (j n) -> p j n", n=4), axis=AX.X, op=ALU.add),
                 reads=[], writes=[rl])
            P.op("vector", lambda e: e.tensor_scalar(out=LNS[:, 2, c0:c1], in0=LNS[:, 0, c0:c1], scalar1=1.0 / 2048, scalar2=None, op0=ALU.mult),
                 reads=[rl], writes=[rl])
            P.op("vector", lambda e: e.tensor_tensor(out=LNS[:, 3, c0:c1], in0=LNS[:, 2, c0:c1], in1=LNS[:, 2, c0:c1], op=ALU.mult),
                 reads=[rl], writes=[rl])
            P.op("vector", lambda e: e.scalar_tensor_tensor(out=LNS[:, 4, c0:c1], in0=LNS[:, 1, c0:c1], scalar=1.0 / 2048, in1=LNS[:, 3, c0:c1],
                                                            op0=ALU.mult, op1=ALU.subtract), reads=[rl], writes=[rl])
            P.op("scalar", lambda e: e.activation(out=LNS[:, 5, c0:c1], in_=LNS[:, 4, c0:c1], func=AF.Sqrt, bias=EPS[:, 1:2], scale=1.0),
                 reads=[rl, rEPS], writes=[rl])
            P.op("vector", lambda e: e.reciprocal(out=LNS[:, 4, c0:c1], in_=LNS[:, 5, c0:c1]), reads=[rl], writes=[rl])
            for j in range(c0, c1):
                P.op("vector", lambda e, j=j: e.tensor_scalar(out=VH[:, j, :], in0=VH[:, j, :], scalar1=LNS[:, 2, j:j + 1], scalar2=LNS[:, 4, j:j + 1],
                                                              op0=ALU.subtract, op1=ALU.mult), reads=[rl], writes=rVH[j])

        def layer1(stx):
            pipelined = (DBG is None) and (stx + 1 < NST)
            for n in range(4):
                s = ring_take(("wv", stx, n))
                for j in range(8):
                    b = cnt["a"] % 4
                    cnt["a"] += 1
                    col = j * 4 + n

                    def mm(e, s=s, b=b, j=j):
                        last = None
                        for k in range(8):
                            last = e.matmul(PS[b][:], lhsT=hT[:, k, j * 128:(j + 1) * 128], rhs=RING[s][:, k, :], start=(k == 0), stop=(k == 7))
                        return last
                    P.op("tensor", mm, reads=[rRING[s], rhT[j // 4]], writes=[rPS[b]])
                    P.op("scalar", lambda e, b=b, n=n, j=j, col=col: e.activation(out=VH[:, j, n * 512:(n + 1) * 512], in_=PS[b][:], func=AF.Gelu,
                                                                                 accum_out=S1[:, col:col + 1]),
                         reads=[rPS[b]], writes=[rVH[j][n // 2]])
                    P.op("vector", lambda e, n=n, j=j, col=col: e.scalar_tensor_tensor(out=TD[0][:], in0=VH[:, j, n * 512:(n + 1) * 512], scalar=1.0,
                                                                                      in1=VH[:, j, n * 512:(n + 1) * 512], op0=ALU.mult, op1=ALU.mult,
                                                                                      accum_out=S2[:, col:col + 1]),
                         reads=[rVH[j][n // 2]], writes=[rJ])
                    if n == 3 and j == 3:
                        ln_half(0)
            ln_half(1)

            ypend = []
            pend = []
            wb_cur = None
            for f in range(16):
                grp, fi = f // 4, f % 4
                g = f // 2
                if fi == 0:
                    wb_cur = wo_load(w1out_d[grp])
                s = ring_take(("wuz", stx, f))
                yb = grp % 2
                if pipelined:
                    next_tile_prep(stx + 1, f)
                def emit_pair(hf):
                    q = cnt["q"] % 2
                    cnt["q"] += 1
                    sl = cnt["pair"] % 2
                    cnt["pair"] += 1
                    p0, p1 = sl * 2, sl * 2 + 1
                    mb = 6 + cnt["mix"] % 2
                    cnt["mix"] += 1

                    def mm(e, s=s, sl=sl, hf=hf):
                        last = None
                        for bi in range(2):
                            for k in range(8):
                                last = e.matmul(PS[sl * 2 + bi][:], lhsT=RING[s][:, k, bi * 128:(bi + 1) * 128],
                                                rhs=hT[:, k, hf * 512:(hf + 1) * 512], start=(k == 0), stop=(k == 7))
                        return last
                    P.op("tensor", mm, reads=[rRING[s], rhT[hf]], writes=[rPS[p0], rPS[p1]])
                    return q, sl, p0, p1, mb

                def emit_rest(hf, st_):
                    q, sl, p0, p1, mb = st_
                    def mix(e, mb=mb, hf=hf, f=f, g=g):
                        last = None
                        for jj in range(4):
                            j = hf * 4 + jj
                            last = e.matmul(PS[mb][:, jj * 128:(jj + 1) * 128], lhsT=VH[:, j, f * 128:(f + 1) * 128], rhs=WT[:, g * 128:(g + 1) * 128],
                                            start=True, stop=True)
                        return last
                    P.op("tensor", mix, reads=[rVH[hf * 4 + jj][f // 8] for jj in range(4)] + [rWT], writes=[rPS[mb]])
                    P.op("scalar", lambda e, q=q, p0=p0: e.activation(out=TA[q][:], in_=PS[p0][:], func=AF.Gelu), reads=[rPS[p0]], writes=[rTA[q]])
                    P.op("scalar", lambda e, q=q, p1=p1: e.activation(out=TB[q][:, 0:512], in_=PS[p1][:], func=AF.Tanh, scale=0.5),
                         reads=[rPS[p1]], writes=[rTB[q]])
                    P.op("vector", lambda e, q=q, mb=mb, f=f: e.scalar_tensor_tensor(
                        out=TC[q][:].rearrange("p (a t) -> p a t", a=4), in0=PS[mb][:].rearrange("p (a t) -> p a t", a=4), scalar=GP[:, f:f + 1],
                        in1=T2[:, f, :].unsqueeze(1).to_broadcast([128, 4, 128]), op0=ALU.mult, op1=ALU.add),
                        reads=[rPS[mb], rT2], writes=[rTC[q]])
                    P.op("vector", lambda e, q=q, p1=p1: e.scalar_tensor_tensor(out=TB[q][:, 0:512], in0=TB[q][:, 0:512], scalar=1.0, in1=PS[p1][:],
                                                                                 op0=ALU.add, op1=ALU.mult),
                         reads=[rPS[p1], rTB[q]], writes=[rTB[q]])
                    P.op("gpsimd", lambda e, q=q: e.tensor_tensor(out=TC[q][:], in0=TA[q][:], in1=TC[q][:], op=ALU.mult),
                         reads=[rTA[q], rTC[q]], writes=[rTC[q]])
                    def y_op(q=q, yb=yb, fi=fi, hf=hf):
                        P.op("vector", lambda e: e.tensor_tensor(out=YT[yb][:, fi, hf * 512:(hf + 1) * 512], in0=TC[q][:], in1=TB[q][:, 0:512], op=ALU.mult),
                             reads=[rTC[q], rTB[q]], writes=[rYT[yb]])
                    if ypend:
                        ypend.pop()()
                    if fi == 3 and hf == 1:
                        y_op()
                    else:
                        ypend.append(y_op)
                    if fi * 2 + hf in WO_SLICE_AT:
                        wo_scale_slice(wb_cur, 1, WO_SLICE_AT[fi * 2 + hf])
                    for _ in range(2):
                        if pend:
                            out_unit(1, *pend.pop(0))
                    if pipelined and f == 15:
                        h_tr_next(hf)

                if f == 0:
                    sts = [emit_pair(0), emit_pair(1)]
                    emit_rest(0, sts[0])
                    emit_rest(1, sts[1])
                else:
                    for hf in range(2):
                        emit_rest(hf, emit_pair(hf))
                if fi == 3:
                    assert not pend
                    pend = [(yb, wb_cur, j, dh) for j in range(8) for dh in range(2)]
            if pipelined:
                return pend
            while pend:
                out_unit(1, *pend.pop(0))
            if DBG is None:
                final_norm(stx, None)
            return []

        def dump_x(stx):
            for j in range(8):
                r0 = stx * T + j * 128
                P.op("sync", lambda e, j=j, r0=r0: e.dma_start(out=out_d[r0:r0 + 128, :], in_=X[j][:]), reads=rX[j], dma=dX[j])

        carry = []
        for stx in range(NST):
            if stx == 0:
                st0_h_phase()
            layer0(stx, carry)
            if DBG == "L0":
                dump_x(stx)
                continue
            h_phase_l1()
            carry = layer1(stx)
            if DBG == "L1":
                dump_x(stx)

        finals = [Ev("dma", sem=d, val=d.count) for d in dX]
        P.emit(nc, final_waits=finals)
    return nc


def _prep_shared(inp):
    f32 = np.float32
    mod_w, mod_b, norm_g = inp["mod_w"], inp["mod_b"], inp["norm_g"]
    d = {}
    d["modw"] = np.ascontiguousarray(mod_w.reshape(2, 8, 128, 6, 512).transpose(0, 3, 2, 1, 4)).reshape(2, 6, 128, 4096)
    d["modb_f"] = np.ascontiguousarray(mod_b.reshape(2, 24, 128).transpose(2, 0, 1)).reshape(128, 48)
    d["modb_g"] = np.ascontiguousarray(mod_b[:, 2048:3072]).reshape(1, 2048)
    d["normg_f"] = np.ascontiguousarray(norm_g.reshape(2, 8, 128).transpose(2, 0, 1)).reshape(128, 16)
    a_w_in = inp["a_w_in"][0]
    d["w0in"] = np.ascontiguousarray(a_w_in.reshape(8, 128, 4, 16, 128).transpose(3, 1, 0, 2, 4)).reshape(16, 128, 4096)
    a_w_out = inp["a_w_out"][0]
    d["w0out"] = np.ascontiguousarray(a_w_out.reshape(4, 4, 128, 1024).transpose(0, 2, 1, 3)).reshape(4, 128, 4096)
    d["convw"] = np.ascontiguousarray(inp["a_conv_w"][0].reshape(3, 16, 128).transpose(2, 1, 0)).reshape(128, 48)
    d["convb"] = np.ascontiguousarray(inp["a_conv_b"][0].reshape(16, 128).T)
    b_w_in = inp["b_w_in"][0]
    uz = np.stack([b_w_in[:, 0:2048], b_w_in[:, 4096:6144]], axis=0)
    d["w1uz"] = np.ascontiguousarray(uz.reshape(2, 8, 128, 16, 128).transpose(3, 2, 1, 0, 4)).reshape(16, 128, 2048)
    wv = b_w_in[:, 2048:4096]
    d["w1v"] = np.ascontiguousarray(wv.reshape(8, 128, 4, 512).transpose(2, 1, 0, 3)).reshape(4, 128, 4096)
    b_w_out = inp["b_w_out"][0]
    d["w1out"] = np.ascontiguousarray(b_w_out.reshape(4, 4, 128, 1024).transpose(0, 2, 1, 3)).reshape(4, 128, 4096)
    d["lng"] = np.ascontiguousarray(inp["b_ln_g"][0].reshape(16, 128).T)
    d["lnb"] = np.ascontiguousarray(inp["b_ln_b"][0].reshape(16, 128).T)
    d["wsT"] = np.ascontiguousarray(inp["b_w_s"][0].transpose(2, 0, 1)).reshape(128, 1024)
    s_idx = np.arange(128)
    d["trilm"] = (s_idx[:, None] <= s_idx[None, :]).astype(f32)
    d["bs_bc"] = np.ascontiguousarray(np.broadcast_to(inp["b_b_s"][0].reshape(1, 1024), (128, 1024)))
    d["fg_bc"] = np.ascontiguousarray(np.broadcast_to(inp["final_g"].reshape(1, 1024), (128, 1024)))
    d["ident"] = np.eye(128, dtype=f32)
    return {k: np.ascontiguousarray(v, dtype=f32) for k, v in d.items()}


def kernel(**inputs):
    inp = {k: np.asarray(v) for k, v in inputs.items()}
    x, c = inp["x"], inp["c"]
    shared = _prep_shared(inp)
    in_maps = []
    ntok = NST * T
    for core in range(8):
        b, hh = core // 2, core % 2
        t0 = hh * ntok
        m = dict(shared)
        m["x"] = np.ascontiguousarray(x[b, t0:t0 + ntok], dtype=np.float32)
        xh = np.zeros((128, 1024), np.float32)
        if hh == 1:
            xh[126:128] = x[b, t0 - 2:t0]
        m["xh"] = xh
        m["hmask"] = np.full((128, 1), 1.0 if hh == 1 else 0.0, np.float32)
        m["c_t"] = np.ascontiguousarray(c[b].reshape(8, 128).T, dtype=np.float32)
        in_maps.append(m)
    nc = build_nc()
    res = run_bass_kernel_spmd(nc, in_maps, core_ids=list(range(8)))
    out = np.empty((4, 8192, 1024), np.float32)
    for core in range(8):
        b, hh = core // 2, core % 2
        out[b, hh * ntok:(hh + 1) * ntok] = res.results[core]["out"]
    return out
```

```python
import numpy as np
from contextlib import ExitStack
import concourse.bass as bass
import concourse.mybir as mybir
from concourse.bass_utils import run_bass_kernel_spmd

F32 = mybir.dt.float32
BF16 = mybir.dt.bfloat16
AF = mybir.ActivationFunctionType
ALU = mybir.AluOpType
AX = mybir.AxisListType

ENGS = ("tensor", "vector", "scalar", "gpsimd", "sync")
NST = 4
DBG = None
T = 1024
RMS_EPS = 1e-6
LN_EPS = 1e-5


class Ev:
    __slots__ = ("kind", "eng", "seq", "sem", "val", "clk", "gi")

    def __init__(self, kind, eng=None, seq=None, sem=None, val=None, clk=None, gi=0):
        self.kind, self.eng, self.seq, self.sem, self.val, self.clk, self.gi = kind, eng, seq, sem, val, clk, gi


class Res:
    __slots__ = ("name", "w", "r")

    def __init__(self, name):
        self.name, self.w, self.r = name, None, []


class DmaSem:
    __slots__ = ("idx", "count")

    def __init__(self, idx):
        self.idx, self.count = idx, 0


class Op:
    __slots__ = ("fn", "deps", "ev", "dma")

    def __init__(self, fn, deps, ev, dma):
        self.fn, self.deps, self.ev, self.dma = fn, deps, ev, dma


class Plan:
    def __init__(self):
        self.ops = {e: [] for e in ENGS}
        self.dmasems = []
        self.known = {e: {f: -1 for f in ENGS} for e in ENGS}
        self.knownd = {e: {} for e in ENGS}
        self.gi = 0

    def dma_sem(self):
        s = DmaSem(len(self.dmasems))
        self.dmasems.append(s)
        return s

    def op(self, eng, fn, reads=(), writes=(), dma=None):
        deps = []
        for r in reads:
            if r.w is not None:
                deps.append(r.w)
        for w in writes:
            if w.w is not None:
                deps.append(w.w)
            deps.extend(w.r)
        kn, knd = self.known[eng], self.knownd[eng]
        fdeps = []
        need_drain = False
        for d in sorted(set(deps), key=lambda d: -d.gi):
            if d.kind == "eng":
                if d.eng == eng:
                    if eng in ("vector", "scalar") and d.seq > kn[eng]:
                        need_drain = True
                    continue
                if d.seq <= kn[d.eng]:
                    continue
                fdeps.append(Ev("eng", eng=d.eng, seq=d.seq))
                kn[d.eng] = d.seq
            else:
                if d.val <= knd.get(d.sem.idx, 0):
                    continue
                fdeps.append(Ev("dma", sem=d.sem, val=d.val))
                knd[d.sem.idx] = d.val
            if d.clk is not None:
                for f, v in d.clk.items():
                    if f != eng and v > kn[f]:
                        kn[f] = v
        seq = len(self.ops[eng])
        if need_drain:
            fdeps.append(Ev("eng", eng=eng, seq=seq - 1))
            kn[eng] = seq - 1
        self.gi += 1
        clk = {f: v for f, v in kn.items() if f != eng and v >= 0}
        if dma is None:
            clk[eng] = seq
            ev = Ev("eng", eng=eng, seq=seq, clk=clk, gi=self.gi)
        else:
            dma.count += 16
            if kn[eng] >= 0:
                clk[eng] = kn[eng]
            ev = Ev("dma", sem=dma, val=dma.count, clk=clk, gi=self.gi)
        self.ops[eng].append(Op(fn, fdeps, ev, dma))
        for r in reads:
            r.r.append(ev)
        for w in writes:
            w.w = ev
            w.r = []
        return ev

    def emit(self, nc, final_waits=()):
        sig = {e: set() for e in ENGS}
        for e in ENGS:
            for o in self.ops[e]:
                for d in o.deps:
                    if d.kind == "eng" and d.eng != e:
                        sig[d.eng].add(d.seq)
        rank = {e: {s: i + 1 for i, s in enumerate(sorted(sig[e]))} for e in ENGS}
        with ExitStack() as st:
            esem = {e: st.enter_context(nc.semaphore("s_" + e)) for e in ENGS}
            dsem = [st.enter_context(nc.semaphore("d%d" % i)) for i in range(len(self.dmasems))]
            block = st.enter_context(nc.Block())

            def run(e, engobj):
                for seq, o in enumerate(self.ops[e]):
                    for d in o.deps:
                        if d.kind == "eng" and d.eng == e:
                            engobj.drain()
                        elif d.kind == "eng":
                            engobj.wait_ge(esem[d.eng], rank[d.eng][d.seq])
                        else:
                            engobj.wait_ge(dsem[d.sem.idx], d.val)
                    inst = o.fn(engobj)
                    if o.dma is not None:
                        inst.then_inc(dsem[o.dma.idx], 16)
                    elif seq in rank[e]:
                        inst.then_inc(esem[e], 1)
                if e == "sync":
                    for d in final_waits:
                        engobj.wait_ge(dsem[d.sem.idx], d.val)

            @block.tensor
            def _(eng):
                run("tensor", eng)

            @block.vector
            def _(eng):
                run("vector", eng)

            @block.scalar
            def _(eng):
                run("scalar", eng)

            @block.gpsimd
            def _(eng):
                run("gpsimd", eng)

            @block.sync
            def _(eng):
                run("sync", eng)


def build_nc():
    nc = bass.Bass("TRN2", target_bir_lowering=False)

    def din(name, shape):
        return nc.dram_tensor(name, list(shape), F32, kind="ExternalInput").ap()

    x_d = din("x", [NST * T, 1024])
    xh_d = din("xh", [128, 1024])
    hmask_d = din("hmask", [128, 1])
    ct_d = din("c_t", [128, 8])
    modw_d = din("modw", [2, 6, 128, 4096])
    modbf_d = din("modb_f", [128, 48])
    modbg_d = din("modb_g", [1, 2048])
    normg_d = din("normg_f", [128, 16])
    w0in_d = din("w0in", [16, 128, 4096])
    w0out_d = din("w0out", [4, 128, 4096])
    convw_d = din("convw", [128, 48])
    convb_d = din("convb", [128, 16])
    w1uz_d = din("w1uz", [16, 128, 2048])
    w1v_d = din("w1v", [4, 128, 4096])
    w1out_d = din("w1out", [4, 128, 4096])
    lng_d = din("lng", [128, 16])
    lnb_d = din("lnb", [128, 16])
    wsT_d = din("wsT", [128, 1024])
    trilm_d = din("trilm", [128, 128])
    bsbc_d = din("bs_bc", [128, 1024])
    fgbc_d = din("fg_bc", [128, 1024])
    ident_d = din("ident", [128, 128])
    out_d = nc.dram_tensor("out", [NST * T, 1024], F32, kind="ExternalOutput").ap()

    with ExitStack() as st:
        st.enter_context(nc.allow_low_precision("bf16 matmul operands, fp32 accumulation"))

        def sb(name, shape, dt):
            return st.enter_context(nc.sbuf_tensor(name, list(shape), dt))

        X = [sb("X%d" % j, [128, 1024], F32) for j in range(8)]
        hT = sb("hT", [128, 8, T], BF16)
        YT = [sb("YT%d" % i, [128, 4, T], BF16) for i in range(2)]
        RING = [sb("RING%d" % i, [128, 8, 512], BF16) for i in range(4)]
        WO = [sb("WO%d" % i, [128, 4, 1024], BF16) for i in range(2)]
        VH = sb("VH", [128, 8, 2048], BF16)
        XN = [sb("XN%d" % i, [128, 1024], BF16) for i in range(4)]
        PST = [sb("PST%d" % i, [128, 1024], F32) for i in range(2)]
        TA = [sb("TA%d" % i, [128, 512], F32) for i in range(2)]
        TB = [sb("TB%d" % i, [128, 516], F32) for i in range(2)]
        TC = [sb("TC%d" % i, [128, 512], F32) for i in range(2)]
        TD = [sb("TD%d" % i, [128, 512], F32) for i in range(2)]
        GATE = sb("GATE", [128, 2, 1024], F32)
        FG = sb("FG", [128, 1024], F32)
        T2 = sb("T2", [128, 16, 128], F32)
        WT = sb("WT", [128, 1024], BF16)
        IDB = sb("IDB", [128, 128], BF16)
        IDF = sb("IDF", [128, 128], F32)
        TRIL = sb("TRIL", [128, 128], F32)
        ONESF = sb("ONESF", [128, 128], F32)
        CT = sb("CT", [128, 8], F32)
        CACT = sb("CACT", [128, 8], BF16)
        MODBF = sb("MODBF", [128, 48], F32)
        NORMG = sb("NORMG", [128, 16], F32)
        CONVW = sb("CONVW", [128, 48], F32)
        CONVB = sb("CONVB", [128, 16], F32)
        LNG = sb("LNG", [128, 16], F32)
        LNB = sb("LNB", [128, 16], F32)
        GP = sb("GP", [128, 16], F32)
        HMASK = sb("HMASK", [128, 1], F32)
        AM = sb("AM", [128, 2, 8], F32)
        SH = sb("SH", [128, 2, 8], F32)
        TM16 = sb("TM16", [128, 16], F32)
        HTH = sb("HTH", [128, 8, 2], BF16)
        TAIL = sb("TAIL", [128, 16, 2], F32)
        CH = sb("CH", [128, 2], F32)
        SS = sb("SS", [128, 4], F32)
        SQ = sb("SQ", [128, 4], F32)
        RS = sb("RS", [128, 4], F32)
        S1 = sb("S1", [128, 32], F32)
        S2 = sb("S2", [128, 32], F32)
        LNS = sb("LNS", [128, 6, 8], F32)
        EPS = sb("EPS", [128, 2], F32)
        SSN = sb("SSN", [128, 8], F32)
        SQN = sb("SQN", [128, 8], F32)
        RSN = sb("RSN", [128, 8], F32)
        SS1 = sb("SS1", [128, 8], F32)
        SQ1 = sb("SQ1", [128, 8], F32)
        RS1 = sb("RS1", [128, 8], F32)
        PS = [st.enter_context(nc.psum_tensor("PS%d" % i, [128, 512], F32)) for i in range(8)]

        P = Plan()
        rX = [[Res("X%d_%d" % (j, d)) for d in range(2)] for j in range(8)]
        rhT = [Res("hT0"), Res("hT1")]
        rhTx = [Res("hT0x"), Res("hT1x")]
        rYT = [Res("YT0"), Res("YT1")]
        rRING = [Res("R%d" % i) for i in range(4)]
        rWO = [Res("WO0"), Res("WO1")]
        rVH = [[Res("VH%d_%d" % (j, c)) for c in range(2)] for j in range(8)]
        rPST = [Res("PST0"), Res("PST1")]
        rSS1 = [Res("SS1a"), Res("SS1b")]
        rSQ1 = [Res("SQ1a"), Res("SQ1b")]
        rRS1 = [Res("RS1a"), Res("RS1b")]
        rSSN, rSQN, rRSN, rJ = Res("SSN"), Res("SQN"), Res("RSN"), Res("JUNK")
        rS2 = Res("S2")
        rXN = [Res("XN%d" % i) for i in range(4)]
        rTA = [Res("TA0"), Res("TA1")]
        rTB = [Res("TB0"), Res("TB1")]
        rTC = [Res("TC0"), Res("TC1")]
        rTD = [Res("TD0"), Res("TD1")]
        rPS = [Res("PS%d" % i) for i in range(8)]
        rC = Res("consts")
        rTAIL, rCH = Res("TAIL"), Res("CH")
        rSS, rSQ, rRS, rS1, rLNS = Res("SS"), Res("SQ"), Res("RS"), Res("S1"), Res("LNS")
        rHTH = Res("HTH")
        rGATE = Res("GATE")
        rMOD = Res("mod")
        rTM16 = Res("TM16")

        dX = [P.dma_sem() for _ in range(8)]
        dRING = [P.dma_sem() for _ in range(4)]
        dWO = [P.dma_sem() for _ in range(2)]
        dPST = [P.dma_sem() for _ in range(2)]

        def cload(dst, src, res_list):
            s = P.dma_sem()
            P.op("sync", lambda e: e.dma_start(out=dst, in_=src), writes=res_list, dma=s)

        SCR = YT[1][:].rearrange("p a b -> p (a b)").bitcast(F32)
        WS_F, BS_BC = SCR[:, 0:1024], SCR[:, 1024:2048]
        GROW = VH[0:1, 0, :].bitcast(F32)
        MODBG = [VH[0:1, 1 + l, :].bitcast(F32) for l in range(2)]

        rc = {k: Res(k) for k in ["CT", "MODBF", "MODBG", "NORMG", "CONVW", "CONVB", "LNG", "LNB", "HMASK", "FG", "TRIL", "IDF"]}
        cload(CT[:], ct_d, [rc["CT"]])
        cload(MODBF[:], modbf_d, [rc["MODBF"]])
        cload(NORMG[:], normg_d, [rc["NORMG"]])
        cload(IDF[:], ident_d, [rc["IDF"]])
        for j in range(8):
            P.op("sync", lambda e, j=j: e.dma_start(out=X[j][:], in_=x_d[j * 128:(j + 1) * 128, :]), writes=rX[j], dma=dX[j])
        cload(PST[0][:], xh_d, [rPST[0]])
        cload(HMASK[:], hmask_d, [rc["HMASK"]])
        for l in range(2):
            cload(MODBG[l], modbg_d[:, l * 1024:(l + 1) * 1024], [rc["MODBG"]] + rVH[1 + l])
        cload(TRIL[:], trilm_d, [rc["TRIL"]])
        cload(WS_F, wsT_d, [rYT[1]])
        cload(BS_BC, bsbc_d, [rYT[1]])
        cload(CONVW[:], convw_d, [rc["CONVW"]])
        cload(CONVB[:], convb_d, [rc["CONVB"]])
        cload(LNG[:], lng_d, [rc["LNG"]])
        cload(LNB[:], lnb_d, [rc["LNB"]])
        cload(FG[:], fgbc_d, [rc["FG"]])

        rONES, rIDB, rEPS, rCACT = Res("ONES"), Res("IDB"), Res("EPS"), Res("CACT")
        P.op("gpsimd", lambda e: e.memset(ONESF[:], 1.0), writes=[rONES])

        def eps_set(e):
            e.memset(EPS[:, 0:1], RMS_EPS)
            return e.memset(EPS[:, 1:2], LN_EPS)
        P.op("gpsimd", eps_set, writes=[rEPS])
        P.op("vector", lambda e: e.tensor_copy(out=IDB[:], in_=IDF[:]), reads=[rc["IDF"]], writes=[rIDB])
        P.op("scalar", lambda e: e.activation(out=CACT[:], in_=CT[:], func=AF.Silu), reads=[rc["CT"]], writes=[rCACT])

        MOD_INS = {1: (0, 4), 2: (0, 5), 4: (1, 0), 6: (1, 1), 8: (1, 2), 10: (1, 3), 12: (1, 4), 14: (1, 5)}
        ring_state = {"issued": 0}
        ring_loads = []

        def ring_full(tag, src):
            return (tag, lambda s: RING[s][:], src.rearrange("p (k c) -> p k c", k=8))

        def ring_half(tag, src):
            return (tag, lambda s: RING[s][:, :, 0:256], src.rearrange("p (k c) -> p k c", k=8))

        for n in range(4):
            ring_loads.append(ring_full(("mod", 0, n), modw_d[0, n]))
        for s_ in range(NST):
            for f in range(16):
                if s_ == 0 and f in MOD_INS:
                    l_, n_ = MOD_INS[f]
                    ring_loads.append(ring_full(("mod", l_, n_), modw_d[l_, n_]))
                ring_loads.append(ring_full(("w0", s_, f), w0in_d[f]))
            for n in range(4):
                ring_loads.append(ring_full(("wv", s_, n), w1v_d[n]))
            for f in range(16):
                ring_loads.append(ring_half(("wuz", s_, f), w1uz_d[f]))

        def ring_ensure(upto):
            upto = min(upto, len(ring_loads) - 1)
            while ring_state["issued"] <= upto:
                i = ring_state["issued"]
                s = i % 4
                _, dstf, src = ring_loads[i]
                P.op("gpsimd", lambda e, dstf=dstf, src=src, s=s: e.dma_start(out=dstf(s), in_=src),
                     writes=[rRING[s]], dma=dRING[s])
                ring_state["issued"] += 1

        ring_pos = {"i": 0}

        def ring_take(tag, ahead=3):
            i = ring_pos["i"]
            assert ring_loads[i][0] == tag, (ring_loads[i][0], tag)
            ring_ensure(i + ahead)
            ring_pos["i"] += 1
            return i % 4

        wo_state = {"n": 0}

        def wo_load(src):
            b = wo_state["n"] % 2
            wo_state["n"] += 1
            P.op("gpsimd", lambda e: e.dma_start(out=WO[b][:], in_=src.rearrange("p (f d) -> p f d", f=4)),
                 writes=[rWO[b]], dma=dWO[b])
            return b

        def wo_scale_slice(b, l, fi):
            P.op("gpsimd", lambda e: e.tensor_tensor(out=WO[b][:, fi, :], in0=WO[b][:, fi, :], in1=GATE[:, l, :], op=ALU.mult),
                 reads=[rGATE], writes=[rWO[b]])

        WO_SLICE_AT = {3: 0, 4: 1, 5: 2, 6: 3}

        def mod_step(l, n, fb, gb):
            s = ring_take(("mod", l, n))
            if n < 4:
                def mm_fm(e, s=s, n=n):
                    last = None
                    for jc in range(4):
                        col = n * 4 + jc
                        for k in range(8):
                            last = e.matmul(PS[fb][:, col:col + 1], lhsT=RING[s][:, k, jc * 128:(jc + 1) * 128],
                                            rhs=CACT[:, k:k + 1], start=(k == 0), stop=(k == 7))
                    return last
                P.op("tensor", mm_fm, reads=[rRING[s], rCACT], writes=[rPS[fb]])
                if n == 3:
                    P.op("vector", lambda e: e.tensor_tensor(out=TM16[:], in0=PS[fb][:, 0:16], in1=MODBF[:, l * 24:l * 24 + 16], op=ALU.add),
                         reads=[rPS[fb], rc["MODBF"]], writes=[rTM16])
                    P.op("vector", lambda e: e.tensor_copy(out=SH[:, l, :], in_=TM16[:, 0:8]), reads=[rTM16], writes=[rMOD])
                    P.op("vector", lambda e: e.scalar_tensor_tensor(out=AM[:, l, :], in0=TM16[:, 8:16], scalar=1.0,
                                                                    in1=NORMG[:, l * 8:(l + 1) * 8], op0=ALU.add, op1=ALU.mult),
                         reads=[rTM16, rc["NORMG"]], writes=[rMOD])
            else:
                dh = n - 4

                def mm_g(e, s=s):
                    last = None
                    for k in range(8):
                        last = e.matmul(PS[gb][0:1, :], lhsT=CACT[:, k:k + 1], rhs=RING[s][:, k, :], start=(k == 0), stop=(k == 7))
                    return last
                P.op("tensor", mm_g, reads=[rRING[s], rCACT], writes=[rPS[gb]])
                P.op("vector", lambda e: e.tensor_tensor(out=GROW[:, dh * 512:(dh + 1) * 512], in0=PS[gb][0:1, :],
                                                         in1=MODBG[l][:, dh * 512:(dh + 1) * 512], op=ALU.add),
                     reads=[rPS[gb], rc["MODBG"]] + rVH[1 + l], writes=rVH[0])
                P.op("tensor", lambda e: e.matmul(PS[gb][:], lhsT=ONESF[0:1, :], rhs=GROW[:, dh * 512:(dh + 1) * 512], start=True, stop=True),
                     reads=[rONES] + rVH[0], writes=[rPS[gb]])
                P.op("scalar", lambda e: e.activation(out=GATE[:, l, dh * 512:(dh + 1) * 512], in_=PS[gb][:], func=AF.Copy),
                     reads=[rPS[gb]], writes=[rGATE])

        rWT = Res("WT")
        rT2 = Res("T2")

        def l1_consts():
            P.op("vector", lambda e: e.tensor_tensor(out=WS_F.rearrange("p (g t) -> p g t", g=8), in0=WS_F.rearrange("p (g t) -> p g t", g=8),
                                                     in1=TRIL[:].unsqueeze(1).to_broadcast([128, 8, 128]), op=ALU.mult),
                 reads=[rc["TRIL"]], writes=[rYT[1]])
            P.op("vector", lambda e: e.tensor_copy(out=WT[:], in_=WS_F), reads=[rYT[1]], writes=[rWT])
            for hh in range(2):
                P.op("tensor", lambda e, hh=hh: e.matmul(PS[7][:], lhsT=ONESF[:], rhs=WS_F[:, hh * 512:(hh + 1) * 512], start=True, stop=True),
                     reads=[rONES, rYT[1]], writes=[rPS[7]])

                def t2_build(e, hh=hh):
                    last = None
                    for f in range(hh * 8, hh * 8 + 8):
                        g = f // 2
                        off = (g % 4) * 128
                        last = e.scalar_tensor_tensor(out=T2[:, f, :], in0=PS[7][:, off:off + 128], scalar=LNB[:, f:f + 1],
                                                      in1=BS_BC[:, g * 128:(g + 1) * 128], op0=ALU.mult, op1=ALU.add)
                    return last
                P.op("vector", t2_build, reads=[rPS[7], rc["LNB"], rYT[1]], writes=[rT2])
            P.op("vector", lambda e: e.tensor_scalar(out=T2[:].rearrange("p f t -> p (f t)"), in0=T2[:].rearrange("p f t -> p (f t)"),
                                                     scalar1=0.5, scalar2=None, op0=ALU.mult), reads=[rT2], writes=[rT2])
            P.op("vector", lambda e: e.tensor_scalar(out=GP[:], in0=LNG[:], scalar1=0.5, scalar2=None, op0=ALU.mult), reads=[rc["LNG"]], writes=[rT2])

        def tpv(b):
            return PS[b][:].bitcast(BF16).rearrange("p (k t) -> p k t", k=2)

        HXN, rHXN = VH[:, 7, 0:1024], rVH[7][0]

        ST0_BANKS = [0, 1, 2, 3, 4, 5, 0, 1]

        def st0_slot(j):
            return (XN[j][:], rXN[j]) if j < 4 else (WO[1][:, j - 4, :], rWO[1])

        def st0_tr(g):
            hf, kp, b = g // 4, g % 4, ST0_BANKS[g]

            def tr(e):
                last = None
                for k2 in range(2):
                    k = kp * 2 + k2
                    for jj in range(4):
                        src, _ = st0_slot(hf * 4 + jj)
                        last = e.transpose(out=tpv(b)[:, k2, jj * 128:(jj + 1) * 128], in_=src[:, k * 128:(k + 1) * 128], identity=IDB[:])
                return last
            P.op("tensor", tr, reads=[st0_slot(hf * 4 + jj)[1] for jj in range(4)] + [rIDB], writes=[rPS[b]])

        def st0_ev(g):
            hf, kp, b = g // 4, g % 4, ST0_BANKS[g]
            k = kp * 2
            P.op("scalar", lambda e: e.activation(out=hT[:, k, hf * 512:(hf + 1) * 512], in_=tpv(b)[:, 0, :], func=AF.Identity,
                                                  scale=AM[:, 0, k:k + 1], bias=SH[:, 0, k:k + 1]),
                 reads=[rPS[b], rMOD], writes=[rhT[hf]])
            P.op("vector", lambda e: e.tensor_scalar(out=hT[:, k + 1, hf * 512:(hf + 1) * 512], in0=tpv(b)[:, 1, :], scalar1=AM[:, 0, k + 1:k + 2],
                                                     scalar2=SH[:, 0, k + 1:k + 2], op0=ALU.mult, op1=ALU.add),
                 reads=[rPS[b], rMOD], writes=[rhTx[hf]])

        def st0_h_phase():
            for hf in range(2):
                for jj in range(4):
                    j = hf * 4 + jj
                    dst, rdst = st0_slot(j)
                    P.op("scalar", lambda e, j=j, dst=dst: e.activation(out=dst, in_=X[j][:], func=AF.Square, accum_out=SS1[:, j:j + 1]),
                         reads=rX[j], writes=[rdst, rSS1[hf]])
                P.op("scalar", lambda e, hf=hf: e.activation(out=SQ1[:, hf * 4:hf * 4 + 4], in_=SS1[:, hf * 4:hf * 4 + 4], func=AF.Sqrt,
                                                             bias=EPS[:, 0:1], scale=1.0 / 1024),
                     reads=[rSS1[hf], rEPS], writes=[rSQ1[hf]])
            for hf in range(2):
                P.op("vector", lambda e, hf=hf: e.reciprocal(out=RS1[:, hf * 4:hf * 4 + 4], in_=SQ1[:, hf * 4:hf * 4 + 4]),
                     reads=[rSQ1[hf]], writes=[rRS1[hf]])
                for jj in range(4):
                    j = hf * 4 + jj
                    dst, rdst = st0_slot(j)
                    P.op("vector", lambda e, j=j, dst=dst: e.tensor_scalar(out=dst, in0=X[j][:], scalar1=RS1[:, j:j + 1], scalar2=None, op0=ALU.mult),
                         reads=rX[j] + [rRS1[hf]], writes=[rdst])
            for g in range(6):
                st0_tr(g)
            halo_pre()
            for n in range(4):
                mod_step(0, n, 6, None)
            halo_ev_op()
            st0_ev(0)
            st0_ev(1)
            st0_tr(6)
            st0_tr(7)
            for g in range(2, 6):
                st0_ev(g)
            st0_ev(6)
            st0_ev(7)

        def halo_pre():
            P.op("scalar", lambda e: e.activation(out=HXN[:], in_=PST[0][:], func=AF.Square, accum_out=SS[:, 0:1]),
                 reads=[rPST[0]], writes=[rHXN, rSS])
            P.op("scalar", lambda e: e.activation(out=SQ[:, 0:1], in_=SS[:, 0:1], func=AF.Sqrt, bias=EPS[:, 0:1], scale=1.0 / 1024),
                 reads=[rSS, rEPS], writes=[rSQ])
            P.op("vector", lambda e: e.reciprocal(out=RS[:, 0:1], in_=SQ[:, 0:1]), reads=[rSQ], writes=[rRS])
            P.op("vector", lambda e: e.tensor_scalar(out=HXN[:], in0=PST[0][:], scalar1=RS[:, 0:1], scalar2=None, op0=ALU.mult),
                 reads=[rPST[0], rRS], writes=[rHXN])

            def halo_tr(e):
                last = None
                for k in range(8):
                    last = e.transpose(out=tpv(7)[:, k // 4, (k % 4) * 128:(k % 4 + 1) * 128], in_=HXN[:, k * 128:(k + 1) * 128], identity=IDB[:])
                return last
            P.op("tensor", halo_tr, reads=[rHXN, rIDB], writes=[rPS[7]])

        def halo_ev_op():
            def halo_ev(e):
                last = None
                for k in range(8):
                    last = e.activation(out=HTH[:, k, :], in_=tpv(7)[:, k // 4, (k % 4) * 128 + 126:(k % 4) * 128 + 128],
                                        func=AF.Identity, scale=AM[:, 0, k:k + 1], bias=SH[:, 0, k:k + 1])
                return last
            P.op("scalar", halo_ev, reads=[rPS[7], rMOD], writes=[rHTH])

        cnt = {"pair": 0, "ou": 0, "q": 0, "mix": 0, "tp": 0, "a": 0, "stg": 0, "pst": 0, "ou2": 0}
        STG = [(TA[0], rTA[0]), (TC[0], rTC[0]), (TD[0], rTD[0]), (TA[1], rTA[1]), (TC[1], rTC[1]), (TD[1], rTD[1])]
        dOUT = [P.dma_sem() for _ in STG]

        def h_phase(stx, l, final=False):
            for hf in range(2):
                for jj in range(4):
                    j = hf * 4 + jj
                    if l == 0 and not final:
                        r0 = stx * T + j * 128
                        P.op("sync", lambda e, j=j, r0=r0: e.dma_start(out=X[j][:], in_=x_d[r0:r0 + 128, :]),
                             writes=rX[j], dma=dX[j])
                    P.op("scalar", lambda e, j=j, jj=jj: e.activation(out=XN[jj][:], in_=X[j][:], func=AF.Square, accum_out=SS[:, jj:jj + 1]),
                         reads=rX[j], writes=[rXN[jj], rSS])
                P.op("scalar", lambda e: e.activation(out=SQ[:], in_=SS[:], func=AF.Sqrt, bias=EPS[:, 0:1], scale=1.0 / 1024),
                     reads=[rSS, rEPS], writes=[rSQ])
                P.op("vector", lambda e: e.reciprocal(out=RS[:], in_=SQ[:]), reads=[rSQ], writes=[rRS])
                if final:
                    for jj in range(4):
                        j = hf * 4 + jj
                        r0 = stx * T + j * 128
                        for dh in range(2):
                            stg, rstg = STG[cnt["stg"] % len(STG)]
                            cnt["stg"] += 1
                            P.op("vector", lambda e, j=j, jj=jj, dh=dh, stg=stg: e.scalar_tensor_tensor(
                                out=stg[:, 0:512], in0=X[j][:, dh * 512:(dh + 1) * 512], scalar=RS[:, jj:jj + 1], in1=FG[:, dh * 512:(dh + 1) * 512],
                                op0=ALU.mult, op1=ALU.mult), reads=rX[j] + [rRS, rc["FG"]], writes=[rstg])
                            P.op("sync", lambda e, r0=r0, dh=dh, stg=stg: e.dma_start(out=out_d[r0:r0 + 128, dh * 512:(dh + 1) * 512], in_=stg[:, 0:512]),
                                 reads=[rstg], dma=dOUT[(cnt["stg"] - 1) % len(STG)])
                    continue
                for jj in range(4):
                    j = hf * 4 + jj
                    P.op("vector", lambda e, j=j, jj=jj: e.tensor_scalar(out=XN[jj][:], in0=X[j][:], scalar1=RS[:, jj:jj + 1], scalar2=None, op0=ALU.mult),
                         reads=rX[j] + [rRS], writes=[rXN[jj]])
                for kp in range(4):
                    b = 6 + cnt["tp"] % 2
                    cnt["tp"] += 1

                    def tr(e, kp=kp, b=b):
                        last = None
                        for k2 in range(2):
                            k = kp * 2 + k2
                            for jj in range(4):
                                last = e.transpose(out=tpv(b)[:, k2, jj * 128:(jj + 1) * 128], in_=XN[jj][:, k * 128:(k + 1) * 128], identity=IDB[:])
                        return last
                    P.op("tensor", tr, reads=rXN + [rIDB], writes=[rPS[b]])

                    def ev(e, kp=kp, b=b, hf=hf, l=l):
                        last = None
                        for k2 in range(2):
                            k = kp * 2 + k2
                            last = e.activation(out=hT[:, k, hf * 512:(hf + 1) * 512], in_=tpv(b)[:, k2, :], func=AF.Identity,
                                                scale=AM[:, l, k:k + 1], bias=SH[:, l, k:k + 1])
                        return last
                    P.op("scalar", ev, reads=[rPS[b], rMOD], writes=[rhT[hf]])

        def next_tile_prep(nstx, f):
            if f < 7:
                return
            if f == 12:
                P.op("scalar", lambda e: e.activation(out=SQN[:], in_=SSN[:], func=AF.Sqrt, bias=EPS[:, 0:1], scale=1.0 / 1024),
                     reads=[rSSN, rEPS], writes=[rSQN])
                P.op("vector", lambda e: e.reciprocal(out=RSN[:], in_=SQN[:]), reads=[rSQN], writes=[rRSN])
            if f >= 8:
                for t in range(2):
                    j = ((f - 8) % 4) * 2 + t
                    if f < 12:
                        P.op("scalar", lambda e, t=t, j=j: e.activation(out=VH[:, j, 0:1024], in_=PST[t][:], func=AF.Square, accum_out=SSN[:, j:j + 1]),
                             reads=[rPST[t]], writes=[rVH[j][0], rSSN])
                    else:
                        P.op("vector", lambda e, t=t, j=j: e.tensor_scalar(out=VH[:, j, 0:1024], in0=PST[t][:], scalar1=RSN[:, j:j + 1], scalar2=None,
                                                                        op0=ALU.mult),
                             reads=[rPST[t], rRSN], writes=[rVH[j][0]])
            if f <= 14:
                for t in range(2):
                    j = ((f + 1 - 8) % 4) * 2 + t
                    r0 = nstx * T + j * 128
                    P.op("sync", lambda e, t=t, r0=r0: e.dma_start(out=PST[t][:], in_=x_d[r0:r0 + 128, :]), writes=[rPST[t]], dma=dPST[t])

        def h_tr_next(hf, l=0):
            for kp in range(4):
                b = 6 + cnt["tp"] % 2
                cnt["tp"] += 1

                def tr(e, kp=kp, b=b):
                    last = None
                    for k2 in range(2):
                        k = kp * 2 + k2
                        for jj in range(4):
                            last = e.transpose(out=tpv(b)[:, k2, jj * 128:(jj + 1) * 128], in_=VH[:, hf * 4 + jj, k * 128:(k + 1) * 128], identity=IDB[:])
                    return last
                P.op("tensor", tr, reads=[rVH[hf * 4 + jj][0] for jj in range(4)] + [rIDB], writes=[rPS[b]])

                def ev(e, kp=kp, b=b):
                    last = None
                    for k2 in range(2):
                        k = kp * 2 + k2
                        last = e.activation(out=hT[:, k, hf * 512:(hf + 1) * 512], in_=tpv(b)[:, k2, :], func=AF.Identity,
                                            scale=AM[:, l, k:k + 1], bias=SH[:, l, k:k + 1])
                    return last
                P.op("scalar", ev, reads=[rPS[b], rMOD], writes=[rhT[hf]])

        def h_phase_l1():
            for hf in range(2):
                for jj in range(4):
                    j = hf * 4 + jj
                    P.op("scalar", lambda e, j=j: e.activation(out=VH[:, j, 1024:2048], in_=X[j][:], func=AF.Square, accum_out=SS1[:, j:j + 1]),
                         reads=rX[j], writes=[rVH[j][1], rSS1[hf]])
                P.op("scalar", lambda e, hf=hf: e.activation(out=SQ1[:, hf * 4:hf * 4 + 4], in_=SS1[:, hf * 4:hf * 4 + 4], func=AF.Sqrt,
                                                             bias=EPS[:, 0:1], scale=1.0 / 1024),
                     reads=[rSS1[hf], rEPS], writes=[rSQ1[hf]])
            for hf in range(2):
                P.op("vector", lambda e, hf=hf: e.reciprocal(out=RS1[:, hf * 4:hf * 4 + 4], in_=SQ1[:, hf * 4:hf * 4 + 4]),
                     reads=[rSQ1[hf]], writes=[rRS1[hf]])
                for jj in range(4):
                    j = hf * 4 + jj
                    P.op("vector", lambda e, j=j: e.tensor_scalar(out=VH[:, j, 0:1024], in0=X[j][:], scalar1=RS1[:, j:j + 1], scalar2=None, op0=ALU.mult),
                         reads=rX[j] + [rRS1[hf]], writes=[rVH[j][0]])
                h_tr_next(hf, 1)

        def final_norm(stx_done, reload_stx, halves=(0, 1)):
            for hf in halves:
                for jj in range(4):
                    j = hf * 4 + jj
                    P.op("scalar", lambda e, j=j, jj=jj: e.activation(out=XN[jj][:], in_=X[j][:], func=AF.Square, accum_out=SS[:, jj:jj + 1]),
                         reads=rX[j], writes=[rXN[jj], rSS])
                P.op("scalar", lambda e: e.activation(out=SQ[:], in_=SS[:], func=AF.Sqrt, bias=EPS[:, 0:1], scale=1.0 / 1024),
                     reads=[rSS, rEPS], writes=[rSQ])
                P.op("vector", lambda e: e.reciprocal(out=RS[:], in_=SQ[:]), reads=[rSQ], writes=[rRS])
                for jj in range(4):
                    j = hf * 4 + jj
                    P.op("vector", lambda e, j=j, jj=jj: e.scalar_tensor_tensor(out=X[j][:], in0=X[j][:], scalar=RS[:, jj:jj + 1], in1=FG[:],
                                                                               op0=ALU.mult, op1=ALU.mult),
                         reads=[rRS, rc["FG"]], writes=rX[j])
                for jj in range(4):
                    j = hf * 4 + jj
                    r0 = stx_done * T + j * 128
                    P.op("sync", lambda e, j=j, r0=r0: e.dma_start(out=out_d[r0:r0 + 128, :], in_=X[j][:]), reads=rX[j], dma=dX[j])
                if reload_stx is not None:
                    for jj in range(4):
                        j = hf * 4 + jj
                        r1 = reload_stx * T + j * 128
                        P.op("sync", lambda e, j=j, r1=r1: e.dma_start(out=X[j][:], in_=x_d[r1:r1 + 128, :]), writes=rX[j], dma=dX[j])

        def out_unit(l, yb, wb, j, dh):
            ob = 4 + cnt["ou"] % 2
            cnt["ou"] += 1

            def mm(e):
                last = None
                for fi in range(4):
                    last = e.matmul(PS[ob][:], lhsT=YT[yb][:, fi, j * 128:(j + 1) * 128], rhs=WO[wb][:, fi, dh * 512:(dh + 1) * 512],
                                    start=(fi == 0), stop=(fi == 3))
                return last
            P.op("tensor", mm, reads=[rYT[yb], rWO[wb]], writes=[rPS[ob]])
            P.op("vector", lambda e: e.tensor_tensor(out=X[j][:, dh * 512:(dh + 1) * 512], in0=X[j][:, dh * 512:(dh + 1) * 512], in1=PS[ob][:], op=ALU.add),
                 reads=[rPS[ob]], writes=[rX[j][dh]])

        def out_unit2(l, a, b_):
            (yb, wb, j, dh0), (yb1, wb1, j1, dh1) = a, b_
            assert (yb, wb, j) == (yb1, wb1, j1) and (dh0, dh1) == (0, 1)
            base = 4 + 2 * (cnt["ou2"] % 2)
            cnt["ou2"] += 1

            def mm(e):
                last = None
                for dh in range(2):
                    for fi in range(4):
                        last = e.matmul(PS[base + dh][:], lhsT=YT[yb][:, fi, j * 128:(j + 1) * 128], rhs=WO[wb][:, fi, dh * 512:(dh + 1) * 512],
                                        start=(fi == 0), stop=(fi == 3))
                return last
            P.op("tensor", mm, reads=[rYT[yb], rWO[wb]], writes=[rPS[base], rPS[base + 1]])
            for dh in range(2):
                P.op("vector", lambda e, dh=dh: e.tensor_tensor(out=X[j][:, dh * 512:(dh + 1) * 512], in0=X[j][:, dh * 512:(dh + 1) * 512],
                                                                in1=PS[base + dh][:], op=ALU.add),
                     reads=[rPS[base + dh]], writes=[rX[j][dh]])

        def layer0(stx, carry):
            npair = 0
            pend = []
            wb_cur = None
            for f in range(16):
                grp, fi = f // 4, f % 4
                if fi == 0:
                    wb_cur = wo_load(w0out_d[grp])
                if stx == 0 and f == 3:
                    l1_consts()
                if stx == 0 and f in MOD_INS:
                    l_, n_ = MOD_INS[f]
                    mod_step(l_, n_, 7, 4 + (n_ - 4) if l_ == 0 else 7)
                s = ring_take(("w0", stx, f))
                yb = grp % 2
                if stx == 0:
                    def hm(e, s=s):
                        last = None
                        for bi, blk in ((0, 1), (1, 2)):
                            for k in range(8):
                                last = e.matmul(PS[6][:, bi * 2:bi * 2 + 2], lhsT=RING[s][:, k, blk * 128:(blk + 1) * 128], rhs=HTH[:, k, :],
                                                start=(k == 0), stop=(k == 7))
                        return last
                    P.op("tensor", hm, reads=[rRING[s], rHTH], writes=[rPS[6]])
                    P.op("scalar", lambda e: e.activation(out=CH[:], in_=PS[6][:, 0:2], func=AF.Copy), reads=[rPS[6]], writes=[rCH])
                    P.op("vector", lambda e, f=f: e.scalar_tensor_tensor(out=TAIL[:, f, :], in0=PS[6][:, 2:4], scalar=HMASK[:, 0:1], in1=CH[:],
                                                                          op0=ALU.mult, op1=ALU.mult),
                         reads=[rPS[6], rCH, rc["HMASK"]], writes=[rTAIL])
                for hf in range(2):
                    q = cnt["q"] % 2
                    cnt["q"] += 1
                    for pi, blks in enumerate((((0, 1), (1, 2)), ((0, 0), (1, 3)))):
                        sl = cnt["pair"] % 2
                        cnt["pair"] += 1

                        def mm(e, s=s, sl=sl, blks=blks, hf=hf):
                            last = None
                            for bi, blk in blks:
                                for k in range(8):
                                    last = e.matmul(PS[sl * 2 + bi][:], lhsT=RING[s][:, k, blk * 128:(blk + 1) * 128],
                                                    rhs=hT[:, k, hf * 512:(hf + 1) * 512], start=(k == 0), stop=(k == 7))
                            return last
                        P.op("tensor", mm, reads=[rRING[s], rhT[hf], rhTx[hf]], writes=[rPS[sl * 2], rPS[sl * 2 + 1]])
                        p0, p1 = sl * 2, sl * 2 + 1
                        if pi == 0:
                            P.op("scalar", lambda e, q=q, p0=p0: e.activation(out=TA[q][:], in_=PS[p0][:], func=AF.Copy),
                                 reads=[rPS[p0]], writes=[rTA[q]])
                            P.op("scalar", lambda e, q=q, f=f: e.activation(out=TB[q][:, 0:2], in_=TAIL[:, f, :], func=AF.Copy),
                                 reads=[rTAIL], writes=[rTB[q]])
                            P.op("vector", lambda e, q=q, p1=p1: e.tensor_tensor(out=TB[q][:, 2:514], in0=PS[p1][:], in1=TA[q][:], op=ALU.mult),
                                 reads=[rPS[p1], rTA[q]], writes=[rTB[q]])
                            P.op("scalar", lambda e, q=q, f=f: e.activation(out=TAIL[:, f, :], in_=TB[q][:, 512:514], func=AF.Copy),
                                 reads=[rTB[q]], writes=[rTAIL])
                            P.op("scalar", lambda e, q=q, f=f: e.activation(out=TC[q][:], in_=TB[q][:, 2:514], func=AF.Identity,
                                                                            scale=CONVW[:, f * 3 + 2:f * 3 + 3], bias=CONVB[:, f:f + 1]),
                                 reads=[rTB[q], rc["CONVW"], rc["CONVB"]], writes=[rTC[q]])

                            P.op("vector", lambda e, q=q, f=f: e.scalar_tensor_tensor(out=TC[q][:], in0=TB[q][:, 1:513], scalar=CONVW[:, f * 3 + 1:f * 3 + 2],
                                                                                     in1=TC[q][:], op0=ALU.mult, op1=ALU.add),
                                 reads=[rTB[q], rTC[q], rc["CONVW"]], writes=[rTC[q]])
                            P.op("vector", lambda e, q=q, f=f: e.scalar_tensor_tensor(out=TC[q][:], in0=TB[q][:, 0:512], scalar=CONVW[:, f * 3:f * 3 + 1],
                                                                                     in1=TC[q][:], op0=ALU.mult, op1=ALU.add),
                                 reads=[rTB[q], rTC[q], rc["CONVW"]], writes=[rTC[q]])
                        else:
                            P.op("scalar", lambda e, q=q, p1=p1: e.activation(out=TD[q][:], in_=PS[p1][:], func=AF.Silu),
                                 reads=[rPS[p1]], writes=[rTD[q]])
                            P.op("vector", lambda e, q=q, p0=p0: e.tensor_tensor(out=TD[q][:], in0=PS[p0][:], in1=TD[q][:], op=ALU.mult),
                                 reads=[rPS[p0], rTD[q]], writes=[rTD[q]])
                            P.op("gpsimd", lambda e, q=q, yb=yb, fi=fi, hf=hf: e.tensor_tensor(out=YT[yb][:, fi, hf * 512:(hf + 1) * 512], in0=TD[q][:], in1=TC[q][:],
                                                                                               op=ALU.mult),
                                 reads=[rTD[q], rTC[q]], writes=[rYT[yb]])
                        if pi == 1:
                            u = fi * 2 + hf
                            if stx == 0 and grp == 0:
                                for sl_ in {5: (0, 1), 6: (2, 3)}.get(u, ()):
                                    wo_scale_slice(wb_cur, 0, sl_)
                            elif u in WO_SLICE_AT:
                                wo_scale_slice(wb_cur, 0, WO_SLICE_AT[u])
                        npair += 1
                        if carry:
                            out_unit2(1, carry.pop(0), carry.pop(0))
                            if len(carry) == 8:
                                final_norm(stx - 1, stx, halves=(0,))
                            if not carry:
                                final_norm(stx - 1, stx, halves=(1,))
                        elif pend:
                            if stx == 0:
                                out_unit(0, *pend.pop(0))
                            elif npair % 2 == 0:
                                out_unit2(0, pend.pop(0), pend.pop(0))
                if fi == 3:
                    assert not pend
                    pend = [(yb, wb_cur, j, dh) for j in range(8) for dh in range(2)]
            assert not carry
            while pend:
                if stx == 0:
                    out_unit(0, *pend.pop(0))
                else:
                    out_unit2(0, pend.pop(0), pend.pop(0))

        rLNSh = [Res("LNSa"), Res("LNSb")]

        def ln_half(h):
            c0, c1 = 4 * h, 4 * h + 4
            rl = rLNSh[h]
            P.op("vector", lambda e: e.tensor_reduce(out=LNS[:, 0, c0:c1], in_=S1[:, c0 * 4:c1 * 4].rearrange("p (j n) -> p j n", n=4), axis=AX.X, op=ALU.add),
                 reads=[r for j in range(c0, c1) for r in rVH[j]], writes=[rl, rVH[c1 - 1][1]])
            P.op("vector", lambda e: e.tensor_reduce(out=LNS[:, 1, c0:c1], in_=S2[:, c0 * 4:c1 * 4].rearrange("p (j n) -> p j n", n=4), axis=AX.X, op=ALU.add),
                 reads=[], writes=[rl])
            P.op("vector", lambda e: e.tensor_scalar(out=LNS[:, 2, c0:c1], in0=LNS[:, 0, c0:c1], scalar1=1.0 / 2048, scalar2=None, op0=ALU.mult),
                 reads=[rl], writes=[rl])
            P.op("vector", lambda e: e.tensor_tensor(out=LNS[:, 3, c0:c1], in0=LNS[:, 2, c0:c1], in1=LNS[:, 2, c0:c1], op=ALU.mult),
                 reads=[rl], writes=[rl])
            P.op("vector", lambda e: e.scalar_tensor_tensor(out=LNS[:, 4, c0:c1], in0=LNS[:, 1, c0:c1], scalar=1.0 / 2048, in1=LNS[:, 3, c0:c1],
                                                            op0=ALU.mult, op1=ALU.subtract), reads=[rl], writes=[rl])
            P.op("scalar", lambda e: e.activation(out=LNS[:, 5, c0:c1], in_=LNS[:, 4, c0:c1], func=AF.Sqrt, bias=EPS[:, 1:2], scale=1.0),
                 reads=[rl, rEPS], writes=[rl])
            P.op("vector", lambda e: e.reciprocal(out=LNS[:, 4, c0:c1], in_=LNS[:, 5, c0:c1]), reads=[rl], writes=[rl])
            for j in range(c0, c1):
                P.op("vector", lambda e, j=j: e.tensor_scalar(out=VH[:, j, :], in0=VH[:, j, :], scalar1=LNS[:, 2, j:j + 1], scalar2=LNS[:, 4, j:j + 1],
                                                              op0=ALU.subtract, op1=ALU.mult), reads=[rl], writes=rVH[j])

        def layer1(stx):
            pipelined = (DBG is None) and (stx + 1 < NST)
            for n in range(4):
                s = ring_take(("wv", stx, n))
                for j in range(8):
                    b = cnt["a"] % 4
                    cnt["a"] += 1
                    col = j * 4 + n

                    def mm(e, s=s, b=b, j=j):
                        last = None
                        for k in range(8):
                            last = e.matmul(PS[b][:], lhsT=hT[:, k, j * 128:(j + 1) * 128], rhs=RING[s][:, k, :], start=(k == 0), stop=(k == 7))
                        return last
                    P.op("tensor", mm, reads=[rRING[s], rhT[j // 4]], writes=[rPS[b]])
                    P.op("scalar", lambda e, b=b, n=n, j=j, col=col: e.activation(out=VH[:, j, n * 512:(n + 1) * 512], in_=PS[b][:], func=AF.Gelu,
                                                                                 accum_out=S1[:, col:col + 1]),
                         reads=[rPS[b]], writes=[rVH[j][n // 2]])
                    P.op("vector", lambda e, n=n, j=j, col=col: e.scalar_tensor_tensor(out=TD[0][:], in0=VH[:, j, n * 512:(n + 1) * 512], scalar=1.0,
                                                                                      in1=VH[:, j, n * 512:(n + 1) * 512], op0=ALU.mult, op1=ALU.mult,
                                                                                      accum_out=S2[:, col:col + 1]),
                         reads=[rVH[j][n // 2]], writes=[rJ])
                    if n == 3 and j == 3:
                        ln_half(0)
            ln_half(1)

            ypend = []
            pend = []
            wb_cur = None
            for f in range(16):
                grp, fi = f // 4, f % 4
                g = f // 2
                if fi == 0:
                    wb_cur = wo_load(w1out_d[grp])
                s = ring_take(("wuz", stx, f))
                yb = grp % 2
                if pipelined:
                    next_tile_prep(stx + 1, f)
                def emit_pair(hf):
                    q = cnt["q"] % 2
                    cnt["q"] += 1
                    sl = cnt["pair"] % 2
                    cnt["pair"] += 1
                    p0, p1 = sl * 2, sl * 2 + 1
                    mb = 6 + cnt["mix"] % 2
                    cnt["mix"] += 1

                    def mm(e, s=s, sl=sl, hf=hf):
                        last = None
                        for bi in range(2):
                            for k in range(8):
                                last = e.matmul(PS[sl * 2 + bi][:], lhsT=RING[s][:, k, bi * 128:(bi + 1) * 128],
                                                rhs=hT[:, k, hf * 512:(hf + 1) * 512], start=(k == 0), stop=(k == 7))
                        return last
                    P.op("tensor", mm, reads=[rRING[s], rhT[hf]], writes=[rPS[p0], rPS[p1]])
                    return q, sl, p0, p1, mb

                def emit_rest(hf, st_):
                    q, sl, p0, p1, mb = st_
                    def mix(e, mb=mb, hf=hf, f=f, g=g):
                        last = None
                        for jj in range(4):
                            j = hf * 4 + jj
                            last = e.matmul(PS[mb][:, jj * 128:(jj + 1) * 128], lhsT=VH[:, j, f * 128:(f + 1) * 128], rhs=WT[:, g * 128:(g + 1) * 128],
                                            start=True, stop=True)
                        return last
                    P.op("tensor", mix, reads=[rVH[hf * 4 + jj][f // 8] for jj in range(4)] + [rWT], writes=[rPS[mb]])
                    P.op("scalar", lambda e, q=q, p0=p0: e.activation(out=TA[q][:], in_=PS[p0][:], func=AF.Gelu), reads=[rPS[p0]], writes=[rTA[q]])
                    P.op("scalar", lambda e, q=q, p1=p1: e.activation(out=TB[q][:, 0:512], in_=PS[p1][:], func=AF.Tanh, scale=0.5),
                         reads=[rPS[p1]], writes=[rTB[q]])
                    P.op("vector", lambda e, q=q, mb=mb, f=f: e.scalar_tensor_tensor(
                        out=TC[q][:].rearrange("p (a t) -> p a t", a=4), in0=PS[mb][:].rearrange("p (a t) -> p a t", a=4), scalar=GP[:, f:f + 1],
                        in1=T2[:, f, :].unsqueeze(1).to_broadcast([128, 4, 128]), op0=ALU.mult, op1=ALU.add),
                        reads=[rPS[mb], rT2], writes=[rTC[q]])
                    P.op("vector", lambda e, q=q, p1=p1: e.scalar_tensor_tensor(out=TB[q][:, 0:512], in0=TB[q][:, 0:512], scalar=1.0, in1=PS[p1][:],
                                                                                 op0=ALU.add, op1=ALU.mult),
                         reads=[rPS[p1], rTB[q]], writes=[rTB[q]])
                    P.op("gpsimd", lambda e, q=q: e.tensor_tensor(out=TC[q][:], in0=TA[q][:], in1=TC[q][:], op=ALU.mult),
                         reads=[rTA[q], rTC[q]], writes=[rTC[q]])
                    def y_op(q=q, yb=yb, fi=fi, hf=hf):
                        P.op("vector", lambda e: e.tensor_tensor(out=YT[yb][:, fi, hf * 512:(hf + 1) * 512], in0=TC[q][:], in1=TB[q][:, 0:512], op=ALU.mult),
                             reads=[rTC[q], rTB[q]], writes=[rYT[yb]])
                    if ypend:
                        ypend.pop()()
                    if fi == 3 and hf == 1:
                        y_op()
                    else:
                        ypend.append(y_op)
                    if fi * 2 + hf in WO_SLICE_AT:
                        wo_scale_slice(wb_cur, 1, WO_SLICE_AT[fi * 2 + hf])
                    for _ in range(2):
                        if pend:
                            out_unit(1, *pend.pop(0))
                    if pipelined and f == 15:
                        h_tr_next(hf)

                if f == 0:
                    sts = [emit_pair(0), emit_pair(1)]
                    emit_rest(0, sts[0])
                    emit_rest(1, sts[1])
                else:
                    for hf in range(2):
                        emit_rest(hf, emit_pair(hf))
                if fi == 3:
                    assert not pend
                    pend = [(yb, wb_cur, j, dh) for j in range(8) for dh in range(2)]
            if pipelined:
                return pend
            while pend:
                out_unit(1, *pend.pop(0))
            if DBG is None:
                final_norm(stx, None)
            return []

        def dump_x(stx):
            for j in range(8):
                r0 = stx * T + j * 128
                P.op("sync", lambda e, j=j, r0=r0: e.dma_start(out=out_d[r0:r0 + 128, :], in_=X[j][:]), reads=rX[j], dma=dX[j])

        carry = []
        for stx in range(NST):
            if stx == 0:
                st0_h_phase()
            layer0(stx, carry)
            if DBG == "L0":
                dump_x(stx)
                continue
            h_phase_l1()
            carry = layer1(stx)
            if DBG == "L1":
                dump_x(stx)

        finals = [Ev("dma", sem=d, val=d.count) for d in dX]
        P.emit(nc, final_waits=finals)
    return nc


def _prep_shared(inp):
    f32 = np.float32
    mod_w, mod_b, norm_g = inp["mod_w"], inp["mod_b"], inp["norm_g"]
    d = {}
    d["modw"] = np.ascontiguousarray(mod_w.reshape(2, 8, 128, 6, 512).transpose(0, 3, 2, 1, 4)).reshape(2, 6, 128, 4096)
    d["modb_f"] = np.ascontiguousarray(mod_b.reshape(2, 24, 128).transpose(2, 0, 1)).reshape(128, 48)
    d["modb_g"] = np.ascontiguousarray(mod_b[:, 2048:3072]).reshape(1, 2048)
    d["normg_f"] = np.ascontiguousarray(norm_g.reshape(2, 8, 128).transpose(2, 0, 1)).reshape(128, 16)
    a_w_in = inp["a_w_in"][0]
    d["w0in"] = np.ascontiguousarray(a_w_in.reshape(8, 128, 4, 16, 128).transpose(3, 1, 0, 2, 4)).reshape(16, 128, 4096)
    a_w_out = inp["a_w_out"][0]
    d["w0out"] = np.ascontiguousarray(a_w_out.reshape(4, 4, 128, 1024).transpose(0, 2, 1, 3)).reshape(4, 128, 4096)
    d["convw"] = np.ascontiguousarray(inp["a_conv_w"][0].reshape(3, 16, 128).transpose(2, 1, 0)).reshape(128, 48)
    d["convb"] = np.ascontiguousarray(inp["a_conv_b"][0].reshape(16, 128).T)
    b_w_in = inp["b_w_in"][0]
    uz = np.stack([b_w_in[:, 0:2048], b_w_in[:, 4096:6144]], axis=0)
    d["w1uz"] = np.ascontiguousarray(uz.reshape(2, 8, 128, 16, 128).transpose(3, 2, 1, 0, 4)).reshape(16, 128, 2048)
    wv = b_w_in[:, 2048:4096]
    d["w1v"] = np.ascontiguousarray(wv.reshape(8, 128, 4, 512).transpose(2, 1, 0, 3)).reshape(4, 128, 4096)
    b_w_out = inp["b_w_out"][0]
    d["w1out"] = np.ascontiguousarray(b_w_out.reshape(4, 4, 128, 1024).transpose(0, 2, 1, 3)).reshape(4, 128, 4096)
    d["lng"] = np.ascontiguousarray(inp["b_ln_g"][0].reshape(16, 128).T)
    d["lnb"] = np.ascontiguousarray(inp["b_ln_b"][0].reshape(16, 128).T)
    d["wsT"] = np.ascontiguousarray(inp["b_w_s"][0].transpose(2, 0, 1)).reshape(128, 1024)
    s_idx = np.arange(128)
    d["trilm"] = (s_idx[:, None] <= s_idx[None, :]).astype(f32)
    d["bs_bc"] = np.ascontiguousarray(np.broadcast_to(inp["b_b_s"][0].reshape(1, 1024), (128, 1024)))
    d["fg_bc"] = np.ascontiguousarray(np.broadcast_to(inp["final_g"].reshape(1, 1024), (128, 1024)))
    d["ident"] = np.eye(128, dtype=f32)
    return {k: np.ascontiguousarray(v, dtype=f32) for k, v in d.items()}


def kernel(**inputs):
    inp = {k: np.asarray(v) for k, v in inputs.items()}
    x, c = inp["x"], inp["c"]
    shared = _prep_shared(inp)
    in_maps = []
    ntok = NST * T
    for core in range(8):
        b, hh = core // 2, core % 2
        t0 = hh * ntok
        m = dict(shared)
        m["x"] = np.ascontiguousarray(x[b, t0:t0 + ntok], dtype=np.float32)
        xh = np.zeros((128, 1024), np.float32)
        if hh == 1:
            xh[126:128] = x[b, t0 - 2:t0]
        m["xh"] = xh
        m["hmask"] = np.full((128, 1), 1.0 if hh == 1 else 0.0, np.float32)
        m["c_t"] = np.ascontiguousarray(c[b].reshape(8, 128).T, dtype=np.float32)
        in_maps.append(m)
    nc = build_nc()
    res = run_bass_kernel_spmd(nc, in_maps, core_ids=list(range(8)))
    out = np.empty((4, 8192, 1024), np.float32)
    for core in range(8):
        b, hh = core // 2, core % 2
        out[b, hh * ntok:(hh + 1) * ntok] = res.results[core]["out"]
    return out
```
